# Optimizing a Trainium2 kernel written in Bass

```python
import jax, jax.numpy as jnp
from jax import lax
import numpy as np

D_MODEL = 2048
BATCH = 4
SEQ = 4096
DEPTH = 4

GRID_W = 64
CTX_LEN = 256
W_A = 1024
LRU_HEADS = 8
LRU_HEAD_DIM = W_A // LRU_HEADS
LRU_C = 8.0
CONV_A_WIDTH = 4
CONV_A_PAD_LEFT = 2
W_B = 1024
POOL_WINDOWS = (2, 4, 8, 16)
N_POOL_GROUPS = len(POOL_WINDOWS)
POOL_GROUP = W_B // N_POOL_GROUPS
W_C = 1024
CONV_C_WIDTH = 31
CONV_C_PAD_LEFT = (CONV_C_WIDTH - 1) // 2
N_BRANCH = 3
BRANCH_W = 1024
EPS = 1e-6
IN_SPLITS = (W_A, W_A, W_B, W_B, W_C, W_C, W_C, N_BRANCH * D_MODEL)
P_IN = sum(IN_SPLITS)

kernel_name = "hybrid_rglru_pool_conformer_dit"


def _rmsnorm(x, g):
    xf = x.astype(jnp.float32)
    y = xf * lax.rsqrt(jnp.mean(xf * xf, axis=-1, keepdims=True) + EPS)
    return (y * g.astype(jnp.float32)).astype(x.dtype)


def _layernorm(x, g, b):
    xf = x.astype(jnp.float32)
    mu = jnp.mean(xf, axis=-1, keepdims=True)
    var = jnp.mean(jnp.square(xf - mu), axis=-1, keepdims=True)
    y = (xf - mu) * lax.rsqrt(var + EPS)
    return (y * g.astype(jnp.float32) + b.astype(jnp.float32)).astype(x.dtype)


def _split_cols(p):
    idx = [int(v) for v in np.cumsum(IN_SPLITS)[:-1]]
    return jnp.split(p, idx, axis=-1)


def _dwconv(x, w, b, pad_left):
    k = w.shape[0]
    y = lax.conv_general_dilated(
        x, w[:, None, :].astype(x.dtype), window_strides=(1,),
        padding=[(pad_left, k - 1 - pad_left)],
        dimension_numbers=("NWC", "WIO", "NWC"),
        feature_group_count=x.shape[-1])
    return y + b.astype(x.dtype)


def _block_diag(u, w, b):
    bs, L, _ = u.shape
    uh = u.reshape(bs, L, LRU_HEADS, LRU_HEAD_DIM)
    y = jnp.einsum("blhi,hij->blhj", uh, w.astype(u.dtype)) + b.astype(u.dtype)
    return y.reshape(bs, L, W_A)


def _lru_coeffs(u, wr, br, wi, bi, lam):
    r = jax.nn.sigmoid(_block_diag(u, wr, br).astype(jnp.float32))
    i = jax.nn.sigmoid(_block_diag(u, wi, bi).astype(jnp.float32))
    log_a = -LRU_C * r * jax.nn.softplus(-lam.astype(jnp.float32))
    a = jnp.exp(log_a)
    mult = jnp.sqrt(-jnp.expm1(2.0 * log_a))
    return a, mult * i * u.astype(jnp.float32)


def _linear_scan(a, b, h0=None):
    if h0 is not None:
        b = b.at[:, 0].add(a[:, 0] * h0)

    def combine(e1, e2):
        a1, b1 = e1
        a2, b2 = e2
        return a1 * a2, a2 * b1 + b2

    _, h = lax.associative_scan(combine, (a, b), axis=1)
    return h


def _rglru_bidir(u_ctx, u_lat, wr, br, wi, bi, lam, need_ctx):
    y_lat = None
    y_ctx = None
    for d in range(2):
        uc = u_ctx if d == 0 else jnp.flip(u_ctx, axis=1)
        ul = u_lat if d == 0 else jnp.flip(u_lat, axis=1)
        a_c, b_c = _lru_coeffs(uc, wr[d], br[d], wi[d], bi[d], lam[d])
        h_c = _linear_scan(a_c, b_c)
        a_l, b_l = _lru_coeffs(ul, wr[d], br[d], wi[d], bi[d], lam[d])
        h_l = _linear_scan(a_l, b_l, h_c[:, -1])
        if d == 1:
            h_l = jnp.flip(h_l, axis=1)
            h_c = jnp.flip(h_c, axis=1)
        y_lat = h_l if y_lat is None else y_lat + h_l
        if need_ctx:
            y_ctx = h_c if y_ctx is None else y_ctx + h_c
    y_lat = y_lat.astype(u_lat.dtype)
    if need_ctx:
        y_ctx = y_ctx.astype(u_ctx.dtype)
    return y_ctx, y_lat


def _multiscale_pool(u, n_rows, w_pool, s_pool):
    bs, L, _ = u.shape
    n = L // n_rows
    uf = u.astype(jnp.float32).reshape(bs, n_rows, n, W_B)
    s = jnp.cumsum(uf, axis=2)
    s = jnp.concatenate([jnp.zeros_like(s[:, :, :1]), s], axis=2)
    t = np.arange(n)
    parts = []
    for k, w in enumerate(POOL_WINDOWS):
        lo = np.maximum(t - w // 2, 0)
        hi = np.minimum(t + w // 2, n)
        cnt = (hi - lo).astype(np.float32)[:, None]
        sk = s[..., k * POOL_GROUP:(k + 1) * POOL_GROUP]
        mean = (jnp.take(sk, hi, axis=2) - jnp.take(sk, lo, axis=2)) / cnt
        parts.append(mean - uf[..., k * POOL_GROUP:(k + 1) * POOL_GROUP])
    p = jnp.stack(parts, axis=-2).astype(u.dtype)
    y = jnp.einsum("brngi,gij->brngj", p, w_pool.astype(u.dtype))
    return y.reshape(bs, L, W_B) * s_pool.astype(u.dtype)


def _conformer_conv(v, g, w_dw, b_dw, ln_g, ln_b, w_pw, b_pw):
    y = v * jax.nn.sigmoid(g)
    y = _dwconv(y, w_dw, b_dw, CONV_C_PAD_LEFT)
    y = jax.nn.silu(_layernorm(y, ln_g, ln_b))
    return y @ w_pw.astype(y.dtype) + b_pw.astype(y.dtype)


def _merge(ys, zs, gl, w_bout, w_out, b_out):
    gates = jax.nn.sigmoid(gl.reshape(gl.shape[:-1] + (N_BRANCH, D_MODEL)))
    acc = None
    for k in range(N_BRANCH):
        t = gates[..., k, :] * ((ys[k] * jax.nn.silu(zs[k])) @ w_bout[k].astype(ys[k].dtype))
        acc = t if acc is None else acc + t
    return acc @ w_out.astype(acc.dtype) + b_out.astype(acc.dtype)


def _stream_out(ya, cols, n_rows, pool_w, pool_scale, convc_w, convc_b, lnc_g, lnc_b, pwc_w, pwc_b,
                w_bout, w_out, b_out):
    _, za, xb, zb, cv, cg, zc, gl = cols
    yb = _multiscale_pool(xb, n_rows, pool_w, pool_scale)
    yc = _conformer_conv(cv, cg, convc_w, convc_b, lnc_g, lnc_b, pwc_w, pwc_b)
    return _merge((ya, yb, yc), (za, zb, zc), gl, w_bout, w_out, b_out)


def setup_inputs(seed: int = 0) -> dict:
    key = jax.random.key(seed)
    ks = jax.random.split(key, 32)
    f32 = jnp.float32

    def nrm(k, shape, scale):
        return jax.random.normal(k, shape, f32) * scale

    u = jax.random.uniform(ks[14], (DEPTH, 2, W_A), f32, minval=0.9, maxval=0.999)
    s = u ** (1.0 / LRU_C)
    lru_lambda = jnp.log(s) - jnp.log1p(-s)
    return {
        "x": nrm(ks[0], (BATCH, SEQ, D_MODEL), 1.0),
        "c": nrm(ks[1], (BATCH, D_MODEL), 1.0),
        "ctx": nrm(ks[2], (BATCH, CTX_LEN, D_MODEL), 1.0),
        "c_ctx": nrm(ks[3], (D_MODEL,), 1.0),
        "norm_g": 1.0 + nrm(ks[4], (DEPTH, D_MODEL), 0.05),
        "w_ada": nrm(ks[5], (DEPTH, D_MODEL, 3 * D_MODEL), 0.5 * D_MODEL ** -0.5),
        "b_ada": nrm(ks[6], (DEPTH, 3 * D_MODEL), 0.02),
        "w_in": nrm(ks[7], (DEPTH, D_MODEL, P_IN), D_MODEL ** -0.5),
        "conv_a_w": nrm(ks[8], (DEPTH, CONV_A_WIDTH, W_A), CONV_A_WIDTH ** -0.5),
        "conv_a_b": nrm(ks[9], (DEPTH, W_A), 0.02),
        "lru_wr": nrm(ks[10], (DEPTH, 2, LRU_HEADS, LRU_HEAD_DIM, LRU_HEAD_DIM), LRU_HEAD_DIM ** -0.5),
        "lru_br": nrm(ks[11], (DEPTH, 2, LRU_HEADS, LRU_HEAD_DIM), 0.02),
        "lru_wi": nrm(ks[12], (DEPTH, 2, LRU_HEADS, LRU_HEAD_DIM, LRU_HEAD_DIM), LRU_HEAD_DIM ** -0.5),
        "lru_bi": nrm(ks[13], (DEPTH, 2, LRU_HEADS, LRU_HEAD_DIM), 0.02),
        "lru_lambda": lru_lambda,
        "pool_w": nrm(ks[15], (DEPTH, N_POOL_GROUPS, POOL_GROUP, POOL_GROUP), POOL_GROUP ** -0.5),
        "pool_scale": 1.0 + nrm(ks[16], (DEPTH, W_B), 0.1),
        "convc_w": nrm(ks[17], (DEPTH, CONV_C_WIDTH, W_C), CONV_C_WIDTH ** -0.5),
        "convc_b": nrm(ks[18], (DEPTH, W_C), 0.02),
        "lnc_g": 1.0 + nrm(ks[19], (DEPTH, W_C), 0.05),
        "lnc_b": nrm(ks[20], (DEPTH, W_C), 0.02),
        "pwc_w": nrm(ks[21], (DEPTH, W_C, W_C), W_C ** -0.5),
        "pwc_b": nrm(ks[22], (DEPTH, W_C), 0.02),
        "w_bout": nrm(ks[23], (DEPTH, N_BRANCH, BRANCH_W, D_MODEL), BRANCH_W ** -0.5),
        "w_out": nrm(ks[24], (DEPTH, D_MODEL, D_MODEL), D_MODEL ** -0.5),
        "b_out": nrm(ks[25], (DEPTH, D_MODEL), 0.02),
        "final_g": 1.0 + nrm(ks[26], (D_MODEL,), 0.05),
    }


def reference(x, c, ctx, c_ctx, norm_g, w_ada, b_ada, w_in, conv_a_w, conv_a_b, lru_wr, lru_br, lru_wi,
              lru_bi, lru_lambda, pool_w, pool_scale, convc_w, convc_b, lnc_g, lnc_b, pwc_w, pwc_b,
              w_bout, w_out, b_out, final_g):
    L = x.shape[1]
    rows = L // GRID_W
    xc = ctx
    for l in range(DEPTH):
        last = l == DEPTH - 1
        mod = jax.nn.silu(c) @ w_ada[l].astype(c.dtype) + b_ada[l].astype(c.dtype)
        shift, scale, gate = jnp.split(mod, 3, axis=-1)
        mod_c = jax.nn.silu(c_ctx) @ w_ada[l].astype(c_ctx.dtype) + b_ada[l].astype(c_ctx.dtype)
        shift_c, scale_c, gate_c = jnp.split(mod_c, 3, axis=-1)

        h = _rmsnorm(x, norm_g[l]) * (1.0 + scale[:, None, :]) + shift[:, None, :]
        hc = _rmsnorm(xc, norm_g[l]) * (1.0 + scale_c) + shift_c

        cols = _split_cols(h @ w_in[l].astype(h.dtype))
        if last:
            xa_c = hc @ w_in[l][:, :W_A].astype(hc.dtype)
            cols_c = None
        else:
            cols_c = _split_cols(hc @ w_in[l].astype(hc.dtype))
            xa_c = cols_c[0]

        ua = _dwconv(cols[0], conv_a_w[l], conv_a_b[l], CONV_A_PAD_LEFT)
        ua_c = _dwconv(xa_c, conv_a_w[l], conv_a_b[l], CONV_A_PAD_LEFT)
        ya_c, ya = _rglru_bidir(ua_c, ua, lru_wr[l], lru_br[l], lru_wi[l], lru_bi[l], lru_lambda[l],
                                need_ctx=not last)

        out = _stream_out(ya, cols, rows, pool_w[l], pool_scale[l], convc_w[l], convc_b[l], lnc_g[l],
                          lnc_b[l], pwc_w[l], pwc_b[l], w_bout[l], w_out[l], b_out[l])
        x = x + gate[:, None, :] * out
        if not last:
            out_c = _stream_out(ya_c, cols_c, 1, pool_w[l], pool_scale[l], convc_w[l], convc_b[l], lnc_g[l],
                                lnc_b[l], pwc_w[l], pwc_b[l], w_bout[l], w_out[l], b_out[l])
            xc = xc + gate_c * out_c
    return _rmsnorm(x, final_g)
```

```python
import contextlib
import numpy as np
import concourse.bass as bass
import concourse.mybir as mybir
from concourse.bass_utils import run_bass_kernel_spmd

F32 = mybir.dt.float32
BF16 = mybir.dt.bfloat16
AF = mybir.ActivationFunctionType
ALU = mybir.AluOpType

D = 2048
NCH = 16
SEQ = 4096
HALF = 2048
CTX = 256
HALO = 16
T = CTX + 2 * HALO + HALF
TM = CTX + HALF
TILES = [(0, CTX + 2 * HALO)] + [(CTX + 2 * HALO + i * 512, 512) for i in range(4)]
TMT = [(0, CTX)] + [(CTX + i * 512, 512) for i in range(4)]
WP = 2384
P_CTX = 16
P_HL = 288
P_OWN = 304
P_HR = 2352
POOL_W = (2, 4, 8, 16)
EPS = 1e-6
NW_IN = 104

SM = {}
_o = 0
for _n, _s in [("norm_g", 16), ("b_out", 16), ("b_ada", 48), ("cwA", 40), ("cbA", 8), ("bg", 32), ("lam", 16),
               ("pool_scale", 8), ("cwC", 248), ("cbC", 8), ("lnc_g", 8), ("lnc_b", 8), ("pwc_b", 8)]:
    SM[_n] = (_o, _s)
    _o += _s
NSM = _o


class Res:
    __slots__ = ("last_w", "reads")

    def __init__(self):
        self.last_w = None
        self.reads = {}


class Eng:
    def __init__(self, name, e, sem):
        self.name = name
        self.e = e
        self.sem = sem
        self.cnt = 0
        self.seen = {}
        self.ring = []
        self.dma_idx = 0


class K:
    def __init__(self, nc, stack):
        self.nc = nc
        self.engs = {}
        for name, e in [("pe", nc.tensor), ("act", nc.scalar), ("dve", nc.vector), ("pool", nc.gpsimd), ("sp", nc.sync)]:
            sem = stack.enter_context(nc.semaphore("s_" + name))
            self.engs[name] = Eng(name, e, sem)
        for name, k in [("sp", 24), ("act", 4), ("pool", 8)]:
            for i in range(k):
                sem = stack.enter_context(nc.semaphore("d_%s%d" % (name, i)))
                self.engs[name].ring.append([sem, 0, "d_%s%d" % (name, i)])
        self.cc_sem = stack.enter_context(nc.semaphore("s_cc"))
        self.cc_cnt = 0
        self.resd = {}

    def R(self, *key):
        r = self.resd.get(key)
        if r is None:
            r = self.resd[key] = Res()
        return r

    @staticmethod
    def _deps(reads, writes):
        toks = {}

        def add(tok):
            if tok is None:
                return
            k = tok[2]
            if k not in toks or toks[k][1] < tok[1]:
                toks[k] = tok
        for r in reads:
            add(r.last_w)
        for w in writes:
            add(w.last_w)
            for tok in w.reads.values():
                add(tok)
        return list(toks.values())

    def _wait(self, eng, toks):
        for tok in toks:
            if tok[2] == eng.name and eng.name == "pe":
                continue
            if eng.seen.get(tok[2], 0) < tok[1]:
                eng.e.wait_ge(tok[0], tok[1])
                eng.seen[tok[2]] = tok[1]

    @staticmethod
    def _mark(tok, reads, writes):
        for r in reads:
            r.reads[tok[2]] = tok
        for w in writes:
            w.last_w = tok
            w.reads = {}

    def op(self, engname, reads, writes, emit):
        eng = self.engs[engname]
        self._wait(eng, self._deps(reads, writes))
        inst = emit(eng.e)
        eng.cnt += 1
        inst.then_inc(eng.sem, 1)
        self._mark((eng.sem, eng.cnt, eng.name), reads, writes)

    def dma(self, out_ap, in_ap, reads, writes, engname="sp"):
        eng = self.engs[engname]
        slot = eng.ring[eng.dma_idx % len(eng.ring)]
        eng.dma_idx += 1
        sem, n, key = slot
        toks = self._deps(reads, writes)
        if n > 0:
            toks.append((sem, 16 * n, key))
        self._wait(eng, toks)
        eng.e.dma_start(out=out_ap, in_=in_ap).then_inc(sem, 16)
        slot[1] = n + 1
        self._mark((sem, 16 * (n + 1), key), reads, writes)

    def coll(self, reads, writes, emit):
        eng = self.engs["pool"]
        self._wait(eng, self._deps(reads, writes))
        inst = emit(eng.e)
        self.cc_cnt += 1
        inst.then_inc(self.cc_sem, 1)
        self._mark((self.cc_sem, self.cc_cnt, "cc"), reads, writes)

    def barrier(self):
        toks = []
        for e in self.engs.values():
            if e.cnt > 0:
                toks.append((e.sem, e.cnt, e.name))
            for sem, n, key in e.ring:
                if n > 0:
                    toks.append((sem, 16 * n, key))
        if self.cc_cnt:
            toks.append((self.cc_sem, self.cc_cnt, "cc"))
        for e in self.engs.values():
            for tok in toks:
                if tok[2] == e.name:
                    continue
                if e.seen.get(tok[2], 0) < tok[1]:
                    e.e.wait_ge(tok[0], tok[1])
                    e.seen[tok[2]] = tok[1]


def build_program(L=4):
    nc = bass.Bass("TRN2", target_bir_lowering=False)
    dt = nc.dram_tensor
    xs = dt("xs", [NCH, 128, T], F32, kind="ExternalInput")
    cvec = dt("cvec", [128, 32], F32, kind="ExternalInput")
    flags = dt("flags", [128, 4], F32, kind="ExternalInput")
    ident_d = dt("ident", [128, 128], F32, kind="ExternalInput")
    rcl_d = dt("rcl", [128, 4 * 64], F32, kind="ExternalInput")
    rcc_d = dt("rcc", [128, 4 * 256], F32, kind="ExternalInput")
    sm_d = dt("sm", [128, L * NSM], F32, kind="ExternalInput")
    fg_d = dt("final_g", [128, 16], F32, kind="ExternalInput")
    w_ada = dt("w_ada", [L, 48, 128, 2048], F32, kind="ExternalInput")
    w_in = dt("w_in", [L, NW_IN, 128, 2048], F32, kind="ExternalInput")
    wg_d = dt("wg", [L, 8, 128, 512], F32, kind="ExternalInput")
    pool_w = dt("pool_w", [L, 8, 128, 256], F32, kind="ExternalInput")
    pwc_w = dt("pwc_w", [L, 8, 128, 1024], F32, kind="ExternalInput")
    w_bout = dt("w_bout", [L, 3, 16, 128, 1024], F32, kind="ExternalInput")
    w_out = dt("w_out", [L, 16, 128, 2048], F32, kind="ExternalInput")
    out = dt("out", [NCH, 128, HALF], F32, kind="ExternalOutput")
    xres = dt("xres", [NCH, 128, T], F32)
    zs = dt("zs", [24, 128, T], BF16)
    gs = dt("gs", [48, 128, T], BF16)
    ys = dt("ys", [24, 128, TM], BF16)
    ycv = dt("ycv", [8, 128, TM], BF16)
    yloc = dt("yloc", [8, 128, HALF], F32)
    afd = dt("afd", [8, 128, HALF], F32)
    abd = dt("abd", [8, 128, HALF], F32)
    cc1s = dt("cc1s", [128, 16], F32)
    cc1d = dt("cc1d", [256, 16], F32)
    hxs = dt("hxs", [2048, 32], F32)
    hxd = dt("hxd", [4096, 32], F32)
    RG = [[0, 1], [2, 3], [4, 5], [6, 7]]

    with contextlib.ExitStack() as stack:
        k = K(nc, stack)
        uid = [0]

        def sb(name, shape, dtype, st=stack):
            uid[0] += 1
            return st.enter_context(nc.sbuf_tensor("sb%d_%s" % (uid[0], name), shape, dtype))
        banks = [stack.enter_context(nc.psum_tensor("ps%d" % i, [128, 512], F32)) for i in range(8)]
        bank_i = [0]

        reserved = set()

        def bank(reserve=False):
            while True:
                i = bank_i[0] % 8
                bank_i[0] += 1
                if i not in reserved:
                    break
            if reserve:
                reserved.add(i)
            return banks[i], k.R("ps", i)

        def unreserve(ps):
            for i in range(8):
                if banks[i] is ps:
                    reserved.discard(i)

        sm = sb("sm", [128, L * NSM], F32)
        cv_sb = sb("cvec", [128, 32], F32)
        scv = sb("scv", [128, 32], BF16)
        fl = sb("flags", [128, 4], F32)
        ident = sb("ident", [128, 128], F32)
        fg = sb("fg", [128, 16], F32)
        ones_d = sb("ones_d", [128, 128], BF16)
        ones_c = sb("ones_c", [128, 128], BF16)
        zcol = sb("zcol", [128, 1], F32)
        modsb = sb("modsb", [128, L, 48, 2], F32)
        der = sb("der", [128, L, 8, 16], F32)
        nls = sb("nls", [128, L, 2, 16], F32)
        st_c = sb("st_c", [128, 16], F32)
        st_o = sb("st_o", [128, 16], F32)
        st_g = sb("st_g", [128, 2, 16], F32)
        st_i = sb("st_i", [128, 16], F32)
        st_t = sb("st_t", [128, 16], F32)
        wst = [sb("wst%d" % i, [128, 2048], F32) for i in range(2)]
        wbf = [sb("wbf%d" % i, [128, 2048], BF16) for i in range(2)]
        etb = [sb("etb%d" % i, [128, 512], BF16) for i in range(3)]
        etf = [sb("etf%d" % i, [128, 512], F32) for i in range(3)]
        cnt = {"w": 0, "etb": 0, "etf": 0}

        def smv(l, name, i=None, n=1):
            o, s = SM[name]
            base = l * NSM + o
            if i is None:
                return sm[:, base:base + s]
            return sm[:, base + i:base + i + n]

        def next_etb():
            i = cnt["etb"] % 3
            cnt["etb"] += 1
            return etb[i], k.R("etb", i)

        def next_etf():
            i = cnt["etf"] % 3
            cnt["etf"] += 1
            return etf[i], k.R("etf", i)

        def load_w(src_ap, ncols, cast=True, cast_eng=None):
            i = cnt["w"] % 2
            cnt["w"] += 1
            k.dma(wst[i][:, 0:ncols], src_ap, [], [k.R("wst", i)])
            if cast_eng is None:
                cast_eng = "act"
            if not cast:
                return wst[i], k.R("wst", i)
            if cast_eng == "act":
                k.op("act", [k.R("wst", i)], [k.R("wbf", i)],
                     lambda e: e.activation(out=wbf[i][:, 0:ncols], in_=wst[i][:, 0:ncols], func=AF.Identity))
            else:
                k.op(cast_eng, [k.R("wst", i)], [k.R("wbf", i)],
                     lambda e: e.tensor_copy(out=wbf[i][:, 0:ncols], in_=wst[i][:, 0:ncols]))
            return wbf[i], k.R("wbf", i)

        def mm_group(ps, psr, lhs_list, rhs_list, reads, n):
            def emit(e):
                inst = None
                nk = len(lhs_list)
                for i in range(nk):
                    inst = e.matmul(ps[:, 0:n], lhsT=lhs_list[i], rhs=rhs_list[i], start=(i == 0), stop=(i == nk - 1))
                return inst
            k.op("pe", reads, [psr], emit)

        k.dma(sm[:], sm_d[:, :], [], [k.R("sm")])
        k.dma(cv_sb[:], cvec[:, :], [], [k.R("cvec")])
        k.dma(fl[:], flags[:, :], [], [k.R("fl")])
        k.dma(ident[:], ident_d[:, :], [], [k.R("ident")])
        k.dma(fg[:], fg_d[:, :], [], [k.R("fg")])
        k.op("dve", [], [k.R("ones_d")], lambda e: e.memset(ones_d[:], 1.0 / D))
        k.op("dve", [], [k.R("ones_c")], lambda e: e.memset(ones_c[:], 1.0 / 1024))
        k.op("dve", [], [k.R("zcol")], lambda e: e.memset(zcol[:], 0.0))
        k.op("act", [k.R("cvec")], [k.R("scv")], lambda e: e.activation(out=scv[:], in_=cv_sb[:], func=AF.Silu))
        for c in range(NCH):
            k.dma(xres[c, :, :], xs[c, :, :], [], [k.R("xres", c, tt) for tt in range(5)])
        mod_ps = {}

        def mod_slice(l, sl, nper=6, n0=None, n1=None):
            if n0 is None:
                n0, n1 = sl * nper, sl * nper + nper
            if n0 == 0:
                mod_ps[l] = bank(reserve=True)
            ps, psr = mod_ps[l]
            for n in range(n0, n1):
                wa, war = load_w(w_ada[l, n, :, :], 2048, cast_eng=("act" if n % 2 == 0 else "dve"))
                wa3 = wa[:].rearrange("p (a b) -> p a b", a=16)

                def emit(e, n=n, wa3=wa3, ps=ps):
                    inst = None
                    for kc in range(16):
                        inst = e.matmul(ps[:, 2 * n:2 * n + 2], lhsT=wa3[:, kc, :], rhs=scv[:, 2 * kc:2 * kc + 2],
                                        start=(kc == 0), stop=(kc == 15))
                    return inst
                k.op("pe", [war, k.R("scv")], [psr], emit)
            if n1 == 48:
                mod_finish(l)

        def mod_finish(l):
            ps, psr = mod_ps[l]
            unreserve(ps)
            ba = smv(l, "b_ada")
            k.op("dve", [psr, k.R("sm")], [k.R("modsb", l)],
                 lambda e, l=l, ps=ps, ba=ba: e.tensor_tensor(
                     out=modsb[:, l, :, :], in0=ps[:, 0:96].rearrange("p (a b) -> p a b", b=2),
                     in1=ba.unsqueeze(2).to_broadcast([128, 48, 2]), op=ALU.add))
            for m in range(2):
                shift = modsb[:, l, 0:16, m]
                scale = modsb[:, l, 16:32, m]
                gate = modsb[:, l, 32:48, m]
                rr = [k.R("modsb", l), k.R("sm")]
                ww = [k.R("der", l)]
                k.op("dve", rr, ww, lambda e, l=l, m=m, scale=scale: e.scalar_tensor_tensor(
                    out=der[:, l, 4 * m + 0, :], in0=scale, scalar=1.0, in1=smv(l, "norm_g"), op0=ALU.add, op1=ALU.mult))
                k.op("dve", rr, ww, lambda e, l=l, m=m, shift=shift: e.tensor_copy(out=der[:, l, 4 * m + 1, :], in_=shift))
                k.op("dve", rr, ww, lambda e, l=l, m=m, gate=gate: e.tensor_copy(out=der[:, l, 4 * m + 2, :], in_=gate))
                k.op("dve", rr, ww, lambda e, l=l, m=m, gate=gate: e.tensor_tensor(
                    out=der[:, l, 4 * m + 3, :], in0=gate, in1=smv(l, "b_out"), op=ALU.mult))

        for l in range(L):
            k.op("act", [k.R("sm")], [k.R("nls", l)],
                 lambda e, l=l: e.activation(out=nls[:, l, 0, :], in_=smv(l, "lam"), func=AF.Exp, scale=-1.0))
            k.op("act", [k.R("nls", l)], [k.R("nls", l)],
                 lambda e, l=l: e.activation(out=nls[:, l, 0, :], in_=nls[:, l, 0, :], func=AF.Ln, bias=1.0))
            k.op("dve", [k.R("nls", l)], [k.R("nls", l)],
                 lambda e, l=l: e.tensor_scalar(out=nls[:, l, 1, :], in0=nls[:, l, 0, :], scalar1=-16.0, scalar2=None, op0=ALU.mult))
            k.op("dve", [k.R("nls", l)], [k.R("nls", l)],
                 lambda e, l=l: e.tensor_scalar(out=nls[:, l, 0, :], in0=nls[:, l, 0, :], scalar1=-8.0, scalar2=None, op0=ALU.mult))
        for sl in range(8):
            mod_slice(0, sl)

        def norm_phase(st, l, final, hT):
            xt = [sb("xt%d" % i, [128, 16, 512], F32, st) for i in range(2)]
            sq = [sb("sq%d" % i, [128, 512], BF16, st) for i in range(2)]
            rs = [sb("rs%d" % i, [128, 512], F32, st) for i in range(2)]
            tiles = list(enumerate(TILES))
            if final:
                tiles = tiles[1:]
            else:
                tiles = tiles[1:] + tiles[:1]
            def xt_load(it_):
                tt_, (t0_, n_) = tiles[it_]
                k.dma(xt[it_ % 2][:, :, 0:n_], xres[:, :, t0_:t0_ + n_].rearrange("c p t -> p c t"),
                      [k.R("xres", c, tt_) for c in range(NCH)], [k.R("xt", it_ % 2)])
            xt_load(0)
            for it, (tt, (t0, n)) in enumerate(tiles):
                b = it % 2
                xtr = k.R("xt", b)
                ps, psr = bank()
                for c in range(NCH):
                    sqr = k.R("sq", c % 2)
                    k.op("act", [xtr], [sqr], lambda e, c=c, b=b, n=n: e.activation(
                        out=sq[c % 2][:, 0:n], in_=xt[b][:, c, 0:n], func=AF.Square))
                    k.op("pe", [sqr, k.R("ones_d")], [psr], lambda e, c=c, ps=ps, n=n: e.matmul(
                        ps[:, 0:n], lhsT=ones_d[:], rhs=sq[c % 2][:, 0:n], start=(c == 0), stop=(c == NCH - 1)))
                if it + 1 < len(tiles):
                    xt_load(it + 1)
                rsr = k.R("rs", b)
                k.op("act", [psr], [rsr], lambda e, b=b, ps=ps, n=n: e.activation(
                    out=rs[b][:, 0:n], in_=ps[:, 0:n], func=AF.Sqrt, bias=EPS))
                k.op("dve", [rsr], [rsr], lambda e, b=b, n=n: e.reciprocal(out=rs[b][:, 0:n], in_=rs[b][:, 0:n]))
                for c in range(NCH):
                    if (not final) and l + 1 < L and c < 7 and it * 7 + c < 32:
                        mod_slice(l + 1, None, n0=it * 7 + c, n1=it * 7 + c + 1)
                    tf, tfr = next_etf()
                    k.op("dve", [xtr, rsr], [tfr], lambda e, c=c, b=b, n=n, tf=tf: e.tensor_tensor(
                        out=tf[:, 0:n], in0=xt[b][:, c, 0:n], in1=rs[b][:, 0:n], op=ALU.mult))
                    if final:
                        of, ofr = next_etf()
                        k.op("act", [tfr, k.R("fg")], [ofr], lambda e, c=c, n=n, tf=tf, of=of: e.activation(
                            out=of[:, 0:n], in_=tf[:, 0:n], func=AF.Identity, scale=fg[:, c:c + 1]))
                        k.dma(out[c, :, (tt - 1) * 512:(tt - 1) * 512 + n], of[:, 0:n], [ofr], [k.R("out", c, tt)])
                    else:
                        hr = k.R("hT", tt)
                        dr = k.R("der", l)
                        if tt == 0:
                            k.op("act", [tfr, dr], [hr], lambda e, c=c, tf=tf: e.activation(
                                out=hT[:, c, 0:CTX], in_=tf[:, 0:CTX], func=AF.Identity,
                                scale=der[:, l, 4, c:c + 1], bias=der[:, l, 5, c:c + 1]))
                            k.op("act", [tfr, dr], [hr], lambda e, c=c, tf=tf: e.activation(
                                out=hT[:, c, CTX:CTX + 32], in_=tf[:, CTX:CTX + 32], func=AF.Identity,
                                scale=der[:, l, 0, c:c + 1], bias=der[:, l, 1, c:c + 1]))
                        else:
                            k.op("act", [tfr, dr], [hr], lambda e, c=c, tf=tf, t0=t0, n=n: e.activation(
                                out=hT[:, c, t0:t0 + n], in_=tf[:, 0:n], func=AF.Identity,
                                scale=der[:, l, 0, c:c + 1], bias=der[:, l, 1, c:c + 1]))

        def layer_front(st, l, last):
            hT = sb("hT", [128, NCH, T], BF16, st)
            with contextlib.ExitStack() as st2:
                norm_phase(st2, l, False, hT)
                k.barrier()
            rows = [sb("row%d" % i, [128, WP], F32, st) for i in range(4)]
            rr_ = [k.R("row", i) for i in range(4)]
            uab = sb("uab", [128, WP], BF16, st)
            glu = sb("glu", [128, WP], BF16, st)
            prow = [sb("prow%d" % i, [128, TM], BF16, st) for i in range(2)]
            wgst = sb("wgst", [128, 512], F32, st)
            wgb = sb("wgb", [128, 512], BF16, st)
            dg = sb("dg", [128, 31, 128], BF16, st)
            pb = [sb("pb%d" % i, [128, 800], F32, st) for i in range(3)]
            pbc = sb("pbc", [128, 304], F32, st)
            pwst = sb("pwst", [128, 256], F32, st)
            yctx_t = sb("yctx_t", [128, CTX], BF16, st)
            rcl = sb("rcl", [128, 4, 64], F32, st)
            rcc = sb("rcc", [128, 4, 256], F32, st)
            k.dma(rcl[:].rearrange("p a b -> p (a b)"), rcl_d[:, :], [], [k.R("rcl")])
            k.dma(rcc[:].rearrange("p a b -> p (a b)"), rcc_d[:, :], [], [k.R("rcc")])
            pwb = [sb("pwb%d" % i, [128, 256], BF16, st) for i in range(2)]
            for i in range(4):
                k.op("pool", [], [rr_[i]], lambda e, i=i: e.memset(rows[i][:], 0.0))
            k.op("pool", [], [k.R("glu")], lambda e: e.memset(glu[:], 0.0))
            k.op("pool", [], [k.R("uab")], lambda e: e.memset(uab[:], 0.0))
            for i in range(3):
                k.op("pool", [], [k.R("pb", i)], lambda e, i=i: e.memset(pb[i][:], 0.0))
            k.op("pool", [], [k.R("pbc")], lambda e: e.memset(pbc[:], 0.0))

            hTr = [k.R("hT", tt) for tt in range(5)]

            def win_matmul(n_idx, evac, skip0=False):
                wb, wbr = pending.pop(0)
                wb3 = wb[:].rearrange("p (a b) -> p a b", a=16)
                for tt, (t0, n) in enumerate(TILES):
                    if skip0 and tt == 0:
                        continue
                    ps, psr = bank()
                    mm_group(ps, psr, [wb3[:, kc, :] for kc in range(16)], [hT[:, kc, t0:t0 + n] for kc in range(16)],
                             [wbr, hTr[tt]], n)
                    evac(tt, t0, n, ps, psr)

            pending = []

            def prefetch(n_idx):
                pending.append(load_w(w_in[l, n_idx, :, :], 2048))

            def evac_store(func, dst, dname, di):
                def f(tt, t0, n, ps, psr):
                    tb, tbr = next_etb()
                    k.op("act", [psr], [tbr], lambda e: e.activation(out=tb[:, 0:n], in_=ps[:, 0:n], func=func))
                    k.dma(dst[di, :, t0:t0 + n], tb[:, 0:n], [tbr], [k.R(dname, di, tt)])
                return f

            def evac_row_padded(row, rowr, dtype_is_bf=False):
                def f(tt, t0, n, ps, psr):
                    if tt == 0:
                        k.op("act", [psr], [rowr], lambda e: e.activation(
                            out=row[:, P_CTX:P_CTX + CTX], in_=ps[:, 0:CTX], func=AF.Identity))
                        k.op("dve", [psr, k.R("fl")], [rowr], lambda e: e.tensor_scalar(
                            out=row[:, P_HL:P_HL + 16], in0=ps[:, CTX:CTX + 16], scalar1=fl[:, 2:3], scalar2=None, op0=ALU.mult))
                        k.op("dve", [psr, k.R("fl")], [rowr], lambda e: e.tensor_scalar(
                            out=row[:, P_HR:P_HR + 16], in0=ps[:, CTX + 16:CTX + 32], scalar1=fl[:, 3:4], scalar2=None, op0=ALU.mult))
                    else:
                        o = P_OWN + (tt - 1) * 512
                        k.op("act", [psr], [rowr], lambda e: e.activation(out=row[:, o:o + 512], in_=ps[:, 0:512], func=AF.Identity))
                return f

            X_, R_, I_, E_ = range(4)
            lo, hi = P_CTX, P_OWN + HALF
            c_lo, c_hi = P_CTX, P_CTX + CTX
            o_lo, o_hi = P_OWN, P_OWN + HALF
            gate_tiles = [(P_CTX, CTX)] + [(P_OWN + i * 512, 512) for i in range(4)]

            def a_dir(h, d, S_):
                smr = k.R("sm")
                nr = k.R("nls", l)
                zb_ = zcol[:, 0:1].to_broadcast([128, HALF])
                for (c0, n) in gate_tiles:
                    for g2 in range(2):
                        q = d * 2 + g2
                        ps, psr = bank()
                        mm_group(ps, psr, [wgb[:, q * 128:(q + 1) * 128]], [uab[:, c0:c0 + n]], [k.R("wgb"), k.R("uab")], n)
                        rq = (R_, I_)[g2]
                        k.op("act", [psr, smr], [rr_[rq]], lambda e, q=q, rq=rq, ps=ps, c0=c0, n=n: e.activation(
                            out=rows[rq][:, c0:c0 + n], in_=ps[:, 0:n], func=AF.Sigmoid, bias=smv(l, "bg", h * 4 + q)))
                s1 = nls[:, l, 0, h * 2 + d:h * 2 + d + 1]
                s2 = nls[:, l, 1, h * 2 + d:h * 2 + d + 1]
                for (a, b) in ((c_lo, c_hi), (o_lo, o_hi)):
                    k.op("act", [rr_[R_], nr], [rr_[S_]], lambda e, a=a, b=b: e.activation(
                        out=rows[S_][:, a:b], in_=rows[R_][:, a:b], func=AF.Exp, scale=s2))
                    k.op("act", [rr_[R_], nr], [rr_[R_]], lambda e, a=a, b=b: e.activation(
                        out=rows[R_][:, a:b], in_=rows[R_][:, a:b], func=AF.Exp, scale=s1))
                    k.op("act", [rr_[S_]], [rr_[S_]], lambda e, a=a, b=b: e.activation(
                        out=rows[S_][:, a:b], in_=rows[S_][:, a:b], func=AF.Sqrt, scale=-1.0, bias=1.0))
                    k.op("dve", [rr_[I_], rr_[S_]], [rr_[I_]], lambda e, a=a, b=b: e.tensor_tensor(
                        out=rows[I_][:, a:b], in0=rows[I_][:, a:b], in1=rows[S_][:, a:b], op=ALU.mult))
                    k.op("dve", [rr_[I_], k.R("uab")], [rr_[I_]], lambda e, a=a, b=b: e.tensor_tensor(
                        out=rows[I_][:, a:b], in0=rows[I_][:, a:b], in1=uab[:, a:b], op=ALU.mult))
                if d == 0:
                    v = lambda r, a, b: rows[r][:, a:b]
                else:
                    v = lambda r, a, b: rows[r][:, a:b][:, ::-1]
                k.op("dve", [rr_[R_], rr_[I_]], [rr_[S_]], lambda e: e.tensor_tensor_scan(
                    out=v(S_, c_lo, c_hi), data0=v(R_, c_lo, c_hi), data1=v(I_, c_lo, c_hi), initial=0.0, op0=ALU.mult, op1=ALU.add))
                k.op("dve", [rr_[R_], rr_[I_]], [rr_[S_]], lambda e: e.tensor_tensor_scan(
                    out=v(S_, o_lo, o_hi), data0=v(R_, o_lo, o_hi), data1=v(I_, o_lo, o_hi), initial=0.0, op0=ALU.mult, op1=ALU.add))
                k.op("dve", [rr_[R_], k.R("zcol")], [rr_[I_]], lambda e: e.tensor_tensor_scan(
                    out=v(I_, o_lo, o_hi), data0=v(R_, o_lo, o_hi), data1=zb_, initial=1.0, op0=ALU.mult, op1=ALU.add))
                cpos = (c_hi - 1) if d == 0 else c_lo
                opos = (o_hi - 1) if d == 0 else o_lo
                si = d * 8 + h
                k.op("dve", [rr_[S_]], [k.R("st_c")], lambda e: e.tensor_copy(out=st_c[:, si:si + 1], in_=rows[S_][:, cpos:cpos + 1]))
                k.op("dve", [rr_[S_], rr_[I_], k.R("st_c")], [k.R("st_o")], lambda e: e.scalar_tensor_tensor(
                    out=st_o[:, si:si + 1], in0=rows[I_][:, opos:opos + 1], scalar=st_c[:, si:si + 1], in1=rows[S_][:, opos:opos + 1],
                    op0=ALU.mult, op1=ALU.add))
                dst = afd if d == 0 else abd
                k.dma(dst[h, :, :], rows[I_][:, o_lo:o_hi], [rr_[I_]], [k.R("afd" if d == 0 else "abd", h)], engname="pool")

            def mixer_a1(h):
                smr = k.R("sm")
                k.dma(wgst[:], wg_d[l, h, :, :], [], [k.R("wgst")])
                k.op("act", [k.R("wgst")], [k.R("wgb")], lambda e: e.activation(out=wgb[:], in_=wgst[:], func=AF.Identity))
                win_matmul(h, evac_row_padded(rows[X_], rr_[X_]))
                cw = lambda m: smv(l, "cwA", h * 5 + m)
                k.op("dve", [rr_[X_], smr], [rr_[E_]], lambda e: e.tensor_scalar(
                    out=rows[E_][:, lo:hi], in0=rows[X_][:, lo - 2:hi - 2], scalar1=cw(0), scalar2=smv(l, "cbA", h),
                    op0=ALU.mult, op1=ALU.add))
                for m in range(1, 5):
                    k.op("dve", [rr_[X_], rr_[E_], smr], [rr_[E_]], lambda e, m=m: e.scalar_tensor_tensor(
                        out=rows[E_][:, lo:hi], in0=rows[X_][:, lo - 2 + m:hi - 2 + m], scalar=cw(m), in1=rows[E_][:, lo:hi],
                        op0=ALU.mult, op1=ALU.add))
                k.op("dve", [rr_[E_]], [k.R("uab")], lambda e: e.tensor_copy(out=uab[:, lo:hi], in_=rows[E_][:, lo:hi]))

            def mixer_a1b(h):
                a_dir(h, 0, E_)

            def mixer_a2(h):
                a_dir(h, 1, X_)
                tb, tbr = yctx_t, k.R("yctx_t")
                k.op("dve", [rr_[E_], rr_[X_]], [tbr], lambda e: e.tensor_tensor(
                    out=tb[:, 0:CTX], in0=rows[E_][:, c_lo:c_hi], in1=rows[X_][:, c_lo:c_hi], op=ALU.add))
                k.dma(ys[h, :, 0:CTX], tb[:, 0:CTX], [tbr], [k.R("ys", h, 0)], engname="pool")
                k.op("dve", [rr_[E_], rr_[X_]], [rr_[E_]], lambda e: e.tensor_tensor(
                    out=rows[E_][:, o_lo:o_hi], in0=rows[E_][:, o_lo:o_hi], in1=rows[X_][:, o_lo:o_hi], op=ALU.add))
                k.dma(yloc[h, :, :], rows[E_][:, o_lo:o_hi], [rr_[E_]], [k.R("yloc", h)], engname="pool")

            def mixer_c(j):
                smr = k.R("sm")
                for tap in range(31):
                    k.op("act", [k.R("ident"), smr], [k.R("dg")], lambda e, tap=tap: e.activation(
                        out=dg[:, tap, :], in_=ident[:], func=AF.Identity, scale=smv(l, "cwC", j * 31 + tap)))
                cvt = {}

                def evac_cv(tt, t0, n, ps, psr):
                    tf, tfr = next_etf()
                    k.op("act", [psr], [tfr], lambda e: e.activation(out=tf[:, 0:n], in_=ps[:, 0:n], func=AF.Identity))
                    cvt[tt] = (tf, tfr)

                def evac_cg(tt, t0, n, ps, psr):
                    gr = k.R("glu")
                    if tt == 0:
                        k.op("act", [psr], [gr], lambda e: e.activation(out=glu[:, P_CTX:P_CTX + CTX], in_=ps[:, 0:CTX], func=AF.Sigmoid))
                        k.op("act", [psr], [gr], lambda e: e.activation(out=glu[:, P_HL:P_HL + 16], in_=ps[:, CTX:CTX + 16], func=AF.Sigmoid))
                        k.op("act", [psr], [gr], lambda e: e.activation(out=glu[:, P_HR:P_HR + 16], in_=ps[:, CTX + 16:CTX + 32], func=AF.Sigmoid))
                    else:
                        o = P_OWN + (tt - 1) * 512
                        k.op("act", [psr], [gr], lambda e: e.activation(out=glu[:, o:o + 512], in_=ps[:, 0:512], func=AF.Sigmoid))

                def evac_cvmul(tt, t0, n, ps, psr):
                    gr = k.R("glu")
                    fr = k.R("fl")
                    if tt == 0:
                        k.op("dve", [psr, gr], [gr], lambda e: e.tensor_tensor(
                            out=glu[:, P_CTX:P_CTX + CTX], in0=ps[:, 0:CTX], in1=glu[:, P_CTX:P_CTX + CTX], op=ALU.mult))
                        k.op("dve", [psr, gr, fr], [gr], lambda e: e.scalar_tensor_tensor(
                            out=glu[:, P_HL:P_HL + 16], in0=ps[:, CTX:CTX + 16], scalar=fl[:, 2:3], in1=glu[:, P_HL:P_HL + 16],
                            op0=ALU.mult, op1=ALU.mult))
                        k.op("dve", [psr, gr, fr], [gr], lambda e: e.scalar_tensor_tensor(
                            out=glu[:, P_HR:P_HR + 16], in0=ps[:, CTX + 16:CTX + 32], scalar=fl[:, 3:4], in1=glu[:, P_HR:P_HR + 16],
                            op0=ALU.mult, op1=ALU.mult))
                    else:
                        o = P_OWN + (tt - 1) * 512
                        k.op("dve", [psr, gr], [gr], lambda e: e.tensor_tensor(
                            out=glu[:, o:o + 512], in0=ps[:, 0:512], in1=glu[:, o:o + 512], op=ALU.mult))

                win_matmul(40 + j, evac_cg)
                win_matmul(32 + j, evac_cvmul)
                for tm, (c0, n) in enumerate([(P_CTX, CTX)] + [(P_OWN + i * 512, 512) for i in range(4)]):
                    ps, psr = bank()
                    mm_group(ps, psr, [dg[:, tap, :] for tap in range(31)],
                             [glu[:, c0 + tap - 15:c0 + tap - 15 + n] for tap in range(31)], [k.R("dg"), k.R("glu")], n)
                    tf, tfr = next_etb()
                    k.op("act", [psr, smr], [tfr], lambda e, ps=ps, n=n, tf=tf: e.activation(
                        out=tf[:, 0:n], in_=ps[:, 0:n], func=AF.Identity, bias=smv(l, "cbC", j)))
                    m0 = TMT[tm][0]
                    k.dma(ycv[j, :, m0:m0 + n], tf[:, 0:n], [tfr], [k.R("ycv", j, tm)])

            def mixer_b(j):
                g = j // 2
                w = POOL_W[g]
                nst = g + 1
                k.dma(pwst[:], pool_w[l, j, :, :], [], [k.R("pwst")])
                k.op("act", [k.R("pwst")], [k.R("pwb", j % 2)], lambda e: e.activation(out=pwb[j % 2][:], in_=pwst[:], func=AF.Identity))

                def evac_pool(tt, t0, n, ps, psr):
                    X, A, B = (pbc if tt == 0 else pb[0]), pb[1], pb[2]
                    xr, ar, br = (k.R("pbc") if tt == 0 else k.R("pb", 0)), k.R("pb", 1), k.R("pb", 2)
                    if tt == 0:
                        nrow, rl, base = 1, CTX, 16
                        span = 288
                    else:
                        nrow, rl, base = 8, 64, 16
                        span = 768
                    rowlen = rl + 32

                    def v(buf, off=0):
                        return buf[:, base + off:base + off + nrow * rowlen].rearrange("p (r c) -> p r c", c=rowlen)[:, :, 0:rl]
                    k.op("dve", [psr], [xr], lambda e: e.tensor_copy(
                        out=v(X), in_=ps[:, 0:nrow * rl].rearrange("p (r c) -> p r c", c=rl)))
                    lo_, hi_ = 8, span - 8
                    k.op("dve", [xr], [ar], lambda e: e.tensor_tensor(
                        out=A[:, lo_:hi_], in0=X[:, lo_ - 1:hi_ - 1], in1=X[:, lo_:hi_], op=ALU.add))
                    cur, curr, oth, othr = A, ar, B, br
                    sh = 1
                    for s in range(1, nst):
                        k.op("dve", [curr], [othr], lambda e, cur=cur, oth=oth, sh=sh: e.tensor_tensor(
                            out=oth[:, lo_:hi_], in0=cur[:, lo_ - sh:hi_ - sh], in1=cur[:, lo_ + sh:hi_ + sh], op=ALU.add))
                        cur, curr, oth, othr = oth, othr, cur, curr
                        sh *= 2
                    rc = (rcc[:, g, :].unsqueeze(1) if tt == 0 else rcl[:, g, :].unsqueeze(1).to_broadcast([128, 8, 64]))
                    k.op("dve", [curr, k.R("rcl"), k.R("rcc")], [othr], lambda e, cur=cur, oth=oth: e.tensor_tensor(
                        out=v(oth), in0=v(cur), in1=rc, op=ALU.mult))
                    pr = k.R("prow", j % 2)
                    m0 = TMT[tt][0]
                    k.op("dve", [othr, xr], [pr], lambda e, oth=oth: e.tensor_tensor(
                        out=prow[j % 2][:, m0:m0 + nrow * rl].rearrange("p (r c) -> p r c", c=rl), in0=v(oth), in1=v(X), op=ALU.subtract))

                def evac_pool_t0(tt, t0, n, ps, psr):
                    evac_pool(tt, t0, n, ps, psr)
                win_matmul(16 + j, evac_pool, skip0=last)
                if j % 2 == 1:
                    for jo in (j - 1, j):
                        wv = pwb[jo % 2][:].rearrange("p (a b) -> p a b", a=2)
                        for tm, (m0, n) in enumerate(TMT):
                            if last and tm == 0:
                                continue
                            ps, psr = bank()
                            mm_group(ps, psr, [wv[:, 0, :], wv[:, 1, :]], [prow[0][:, m0:m0 + n], prow[1][:, m0:m0 + n]],
                                     [k.R("pwb", jo % 2), k.R("prow", 0), k.R("prow", 1)], n)
                            tb, tbr = next_etb()
                            k.op("act", [psr, k.R("sm")], [tbr], lambda e, ps=ps, n=n, tb=tb, jo=jo: e.activation(
                                out=tb[:, 0:n], in_=ps[:, 0:n], func=AF.Identity, scale=smv(l, "pool_scale", jo)))
                            k.dma(ys[8 + jo, :, m0:m0 + n], tb[:, 0:n], [tbr], [k.R("ys", 8 + jo, tm)])

            jobs = []
            for j in range(8):
                jobs.append(("A1", j, [j]))
                for q in range(0, 2):
                    jobs.append(("G", j * 6 + q, [56 + j * 6 + q]))
                jobs.append(("A1b", j, []))
                for q in range(2, 4):
                    jobs.append(("G", j * 6 + q, [56 + j * 6 + q]))
                jobs.append(("C", j, [40 + j, 32 + j]))
                jobs.append(("A2", j, []))
                for q in range(4, 6):
                    jobs.append(("G", j * 6 + q, [56 + j * 6 + q]))
                jobs.append(("B", j, [16 + j]))
                for q, base in enumerate((8, 24, 48)):
                    jobs.append(("Z", q * 8 + j, [base + j]))
            wl = [n for jb in jobs for n in jb[2]]
            wpos = [0]

            def pf():
                if wpos[0] < len(wl):
                    prefetch(wl[wpos[0]])
                    wpos[0] += 1
            pf()
            _orig_win = win_matmul

            def win_matmul(n_idx, evac, skip0=False):
                pf()
                _orig_win(n_idx, evac, skip0)
            for kind, idx, ns in jobs:
                if kind == "A1":
                    mixer_a1(idx)
                elif kind == "A1b":
                    mixer_a1b(idx)
                elif kind == "A2":
                    mixer_a2(idx)
                    if idx == 7:
                        k.dma(cc1s[:, :], st_o[:], [k.R("st_o")], [k.R("cc1s")], engname="pool")
                        k.coll([k.R("cc1s")], [k.R("cc1d")], lambda e: e.collective_compute(
                            "AllGather", ALU.bypass, replica_groups=RG, ins=[cc1s.ap().opt()], outs=[cc1d.ap().opt()]))
                        k.dma(st_g[:], cc1d.ap().rearrange("(r p) f -> p r f", p=128), [k.R("cc1d")], [k.R("st_g")], engname="pool")
                elif kind == "G":
                    win_matmul(ns[0], evac_store(AF.Sigmoid, gs, "gs", idx), skip0=last)
                elif kind == "Z":
                    win_matmul(ns[0], evac_store(AF.Silu, zs, "zs", idx), skip0=last)
                elif kind == "C":
                    mixer_c(idx)
                elif kind == "B":
                    mixer_b(idx)

            gr_ = [k.R("st_g"), k.R("st_c"), k.R("fl")]
            k.op("dve", gr_, [k.R("st_t")], lambda e: e.tensor_tensor(out=st_t[:, 0:8], in0=st_c[:, 0:8], in1=st_g[:, 0, 0:8], op=ALU.subtract))
            k.op("dve", gr_, [k.R("st_t")], lambda e: e.tensor_tensor(out=st_t[:, 8:16], in0=st_c[:, 8:16], in1=st_g[:, 1, 8:16], op=ALU.subtract))
            k.op("dve", gr_ + [k.R("st_t")], [k.R("st_i")], lambda e: e.scalar_tensor_tensor(
                out=st_i[:, 0:8], in0=st_t[:, 0:8], scalar=fl[:, 0:1], in1=st_g[:, 0, 0:8], op0=ALU.mult, op1=ALU.add))
            k.op("dve", gr_ + [k.R("st_t")], [k.R("st_i")], lambda e: e.scalar_tensor_tensor(
                out=st_i[:, 8:16], in0=st_t[:, 8:16], scalar=fl[:, 1:2], in1=st_g[:, 1, 8:16], op0=ALU.mult, op1=ALU.add))

        def post_phase(l, last):
            tiles = list(enumerate(TMT))
            if last:
                tiles = tiles[1:]
            smr = k.R("sm")
            with contextlib.ExitStack() as so:
                pbuf = [sb("p0", [128, 8, TM], BF16, so)]
                yb = [sb("yb%d" % i, [128, 512], BF16, so) for i in range(2)]
                zb = [sb("zb%d" % i, [128, 512], BF16, so) for i in range(2)]
                ci = [0, 0, 0]

                def p_build_ops(kb, slot):
                    ops = []
                    for j in range(8):
                        for tm, (m0, n) in tiles:
                            def f(j=j, tm=tm, m0=m0, n=n):
                                p = pbuf[slot]
                                i = ci[0] % 2
                                ci[0] += 1
                                t0 = TILES[tm][0]
                                k.dma(yb[i][:, 0:n], ys[kb * 8 + j, :, m0:m0 + n], [k.R("ys", kb * 8 + j, tm)], [k.R("yb", i)])
                                k.dma(zb[i][:, 0:n], zs[kb * 8 + j, :, t0:t0 + n], [k.R("zs", kb * 8 + j, tm)], [k.R("zb", i)])
                                k.op("pool", [k.R("yb", i), k.R("zb", i)], [k.R("p", slot, tm)], lambda e: e.tensor_tensor(
                                    out=p[:, j, m0:m0 + n], in0=yb[i][:, 0:n], in1=zb[i][:, 0:n], op=ALU.mult))
                            ops.append(f)
                    return ops

                with contextlib.ExitStack() as st:
                    yt = [sb("yt%d" % i, [128, 8, 512], BF16, st) for i in range(2)]
                    sq = [sb("lsq%d" % i, [128, 512], BF16, st) for i in range(2)]
                    msb = [sb("msb%d" % i, [128, 512], F32, st) for i in range(2)]
                    vsb = [sb("vsb%d" % i, [128, 512], F32, st) for i in range(2)]
                    yn = [sb("yn%d" % i, [128, 8, 512], BF16, st) for i in range(2)]
                    pw = sb("pw", [128, 8, 1024], BF16, st)
                    ra = [[sb("ra%d_%d" % (q, i), [128, HALF], F32, st) for i in range(3)] for q in range(2)]
                    rb = [sb("rb%d" % q, [128, HALF], BF16, st) for q in range(2)]
                    for jo in range(8):
                        wb, wbr = load_w(pwc_w[l, jo, :, :], 1024, cast=False)
                        k.op("dve", [wbr], [k.R("pw")], lambda e, jo=jo, wb=wb: e.tensor_copy(out=pw[:, jo, :], in_=wb[:, 0:1024]))

                    def stage2(h):
                        q = h % 2
                        r0, r1, r2 = ra[q]
                        k.dma(r0[:], yloc[h, :, :], [k.R("yloc", h)], [k.R("ra", q, 0)], engname="pool")
                        k.dma(r1[:], afd[h, :, :], [k.R("afd", h)], [k.R("ra", q, 1)], engname="pool")
                        k.dma(r2[:], abd[h, :, :], [k.R("abd", h)], [k.R("ra", q, 2)], engname="pool")
                        k.op("dve", [k.R("ra", q, 0), k.R("ra", q, 1), k.R("st_i")], [k.R("ra", q, 0)], lambda e: e.scalar_tensor_tensor(
                            out=r0[:], in0=r1[:], scalar=st_i[:, h:h + 1], in1=r0[:], op0=ALU.mult, op1=ALU.add))
                        k.op("dve", [k.R("ra", q, 0), k.R("ra", q, 2), k.R("st_i")], [k.R("rb", q)], lambda e: e.scalar_tensor_tensor(
                            out=rb[q][:], in0=r2[:], scalar=st_i[:, 8 + h:9 + h], in1=r0[:], op0=ALU.mult, op1=ALU.add))
                        k.dma(ys[h, :, CTX:TM], rb[q][:], [k.R("rb", q)], [k.R("ys", h, tm) for tm in range(1, 5)], engname="pool")

                    lnst = {}

                    def ln_a(tm):
                        m0, n = TMT[tm]
                        b = tm % 2
                        ytr = k.R("yt", b)
                        k.dma(yt[b][:, :, 0:n], ycv[:, :, m0:m0 + n].rearrange("c p t -> p c t"), [k.R("ycv", j, tm) for j in range(8)], [ytr])
                        pm, pmr = bank()
                        pq, pqr = bank()
                        for j in range(8):
                            sqr = k.R("lsq", j % 2)
                            k.op("act", [ytr], [sqr], lambda e, j=j: e.activation(out=sq[j % 2][:, 0:n], in_=yt[b][:, j, 0:n], func=AF.Square))
                            k.op("pe", [ytr, k.R("ones_c")], [pmr], lambda e, j=j: e.matmul(
                                pm[:, 0:n], lhsT=ones_c[:], rhs=yt[b][:, j, 0:n], start=(j == 0), stop=(j == 7)))
                            k.op("pe", [sqr, k.R("ones_c")], [pqr], lambda e, j=j: e.matmul(
                                pq[:, 0:n], lhsT=ones_c[:], rhs=sq[j % 2][:, 0:n], start=(j == 0), stop=(j == 7)))
                        lnst[tm] = (pm, pmr, pq, pqr)

                    def ln_a2(tm):
                        m0, n = TMT[tm]
                        b = tm % 2
                        pm, pmr, pq, pqr = lnst[tm]
                        mr, vr = k.R("msb", b), k.R("vsb", b)
                        ms_, vs_ = msb[b], vsb[b]
                        k.op("act", [pmr], [mr], lambda e: e.activation(out=ms_[:, 0:n], in_=pm[:, 0:n], func=AF.Identity))
                        k.op("dve", [mr], [vr], lambda e: e.tensor_tensor(out=vs_[:, 0:n], in0=ms_[:, 0:n], in1=ms_[:, 0:n], op=ALU.mult))
                        k.op("dve", [pqr, vr], [vr], lambda e: e.tensor_tensor(out=vs_[:, 0:n], in0=pq[:, 0:n], in1=vs_[:, 0:n], op=ALU.subtract))
                        k.op("dve", [vr], [vr], lambda e: e.tensor_scalar(out=vs_[:, 0:n], in0=vs_[:, 0:n], scalar1=0.0, scalar2=None, op0=ALU.max))
                        k.op("act", [vr], [vr], lambda e: e.activation(out=vs_[:, 0:n], in_=vs_[:, 0:n], func=AF.Sqrt, bias=EPS))
                        k.op("dve", [vr], [vr], lambda e: e.reciprocal(out=vs_[:, 0:n], in_=vs_[:, 0:n]))

                    def ln_b(tm):
                        m0, n = TMT[tm]
                        b = tm % 2
                        ytr, mr, vr, ynr = k.R("yt", b), k.R("msb", b), k.R("vsb", b), k.R("yn", b)
                        ms_, vs_ = msb[b], vsb[b]
                        for j in range(8):
                            tf, tfr = next_etf()
                            k.op("dve", [ytr, mr], [tfr], lambda e, j=j, tf=tf: e.tensor_tensor(
                                out=tf[:, 0:n], in0=yt[b][:, j, 0:n], in1=ms_[:, 0:n], op=ALU.subtract))
                            k.op("dve", [tfr, vr], [tfr], lambda e, tf=tf: e.tensor_tensor(
                                out=tf[:, 0:n], in0=tf[:, 0:n], in1=vs_[:, 0:n], op=ALU.mult))
                            k.op("act", [tfr, smr], [ynr], lambda e, j=j, tf=tf: e.activation(
                                out=yn[b][:, j, 0:n], in_=tf[:, 0:n], func=AF.Silu, scale=smv(l, "lnc_g", j), bias=smv(l, "lnc_b", j)))

                    def ln_b2(tm):
                        m0, n = TMT[tm]
                        b = tm % 2
                        ynr = k.R("yn", b)
                        for jo in range(8):
                            ps, psr = bank()
                            pwv = pw[:, jo, :].rearrange("p (a b) -> p a b", a=8)
                            mm_group(ps, psr, [pwv[:, kc, :] for kc in range(8)], [yn[b][:, kc, 0:n] for kc in range(8)], [k.R("pw"), ynr], n)
                            tb, tbr = next_etb()
                            k.op("act", [psr, smr], [tbr], lambda e, ps=ps, tb=tb, jo=jo: e.activation(
                                out=tb[:, 0:n], in_=ps[:, 0:n], func=AF.Identity, bias=smv(l, "pwc_b", jo)))
                            k.dma(ys[16 + jo, :, m0:m0 + n], tb[:, 0:n], [tbr], [k.R("ys", 16 + jo, tm)])

                    ln_tiles = [tm for tm, _ in tiles]
                    pb_ops = p_build_ops(1, 0)
                    per_t = (len(pb_ops) + len(ln_tiles) - 1) // len(ln_tiles)
                    ln_a(ln_tiles[0])
                    ln_a2(ln_tiles[0])
                    hq = list(range(8))
                    for ii, tm in enumerate(ln_tiles):
                        if ii + 1 < len(ln_tiles):
                            ln_a(ln_tiles[ii + 1])
                        ln_b(tm)
                        if ii + 1 < len(ln_tiles):
                            ln_a2(ln_tiles[ii + 1])
                        ln_b2(tm)
                        for h in hq[ii * 2:ii * 2 + 2]:
                            stage2(h)
                        if l + 1 < L:
                            for n_ in range(32 + ii * 4, min(48, 32 + ii * 4 + 4)):
                                mod_slice(l + 1, None, n0=n_, n1=n_ + 1)
                        for f in pb_ops[ii * per_t:(ii + 1) * per_t]:
                            f()
                    for h in hq[len(ln_tiles) * 2:]:
                        stage2(h)
                    k.barrier()

                with contextlib.ExitStack() as st:
                    pbuf.append(sb("p1", [128, 8, TM], BF16, st))
                    acc = sb("acc", [128, NCH, TM], BF16, st)
                    gb = [sb("gb%d" % i, [128, 512], BF16, st) for i in range(3)]
                    xb = [sb("xb%d" % i, [128, 512], F32, st) for i in range(4)]
                    order = [(1, 0), (0, 1), (2, 0)]
                    for oi, (kb, slot) in enumerate(order):
                        p = pbuf[slot]
                        nxt_ops = p_build_ops(*order[oi + 1]) if oi + 1 < len(order) else []
                        per_c = (len(nxt_ops) + NCH - 1) // NCH
                        nxt = load_w(w_bout[l, kb, 0, :, :], 1024, cast_eng="act")
                        for c in range(NCH):
                            wb, wbr = nxt
                            if c + 1 < NCH:
                                nxt = load_w(w_bout[l, kb, c + 1, :, :], 1024, cast_eng="act")
                            wv = wb[:, 0:1024].rearrange("p (a b) -> p a b", a=8)
                            for tm, (m0, n) in tiles:
                                i = ci[1] % 3
                                ci[1] += 1
                                t0 = TILES[tm][0]
                                k.dma(gb[i][:, 0:n], gs[kb * 16 + c, :, t0:t0 + n], [k.R("gs", kb * 16 + c, tm)], [k.R("gb", i)])
                                ps, psr = bank()
                                mm_group(ps, psr, [wv[:, kc, :] for kc in range(8)], [p[:, kc, m0:m0 + n] for kc in range(8)],
                                         [wbr, k.R("p", slot, tm)], n)
                                ar = k.R("acc", c, tm)
                                if oi == 0:
                                    k.op("dve", [psr, k.R("gb", i)], [ar], lambda e, ps=ps, i=i, c=c, m0=m0, n=n: e.tensor_tensor(
                                        out=acc[:, c, m0:m0 + n], in0=ps[:, 0:n], in1=gb[i][:, 0:n], op=ALU.mult))
                                else:
                                    tf, tfr = next_etf()
                                    k.op("dve", [psr, k.R("gb", i)], [tfr], lambda e, ps=ps, i=i, tf=tf, n=n: e.tensor_tensor(
                                        out=tf[:, 0:n], in0=ps[:, 0:n], in1=gb[i][:, 0:n], op=ALU.mult))
                                    k.op("dve", [tfr, ar], [ar], lambda e, tf=tf, c=c, m0=m0, n=n: e.tensor_tensor(
                                        out=acc[:, c, m0:m0 + n], in0=tf[:, 0:n], in1=acc[:, c, m0:m0 + n], op=ALU.add))
                            for f in nxt_ops[c * per_c:(c + 1) * per_c]:
                                f()
                    its = [(c, tm, m0, n) for c in range(NCH) for tm, (m0, n) in tiles]

                    def xload(it):
                        c, tm, m0, n = its[it]
                        t0 = 0 if tm == 0 else TILES[tm][0]
                        k.dma(xb[it % 4][:, 0:n], xres[c, :, t0:t0 + n], [k.R("xres", c, tm)], [k.R("xb", it % 4)])
                    xload(0)
                    xload(1)
                    nxt = load_w(w_out[l, 0, :, :], 2048, cast_eng="act")
                    for it, (c, tm, m0, n) in enumerate(its):
                        if tm == tiles[0][0]:
                            wb, wbr = nxt
                            if c + 1 < NCH:
                                nxt = load_w(w_out[l, c + 1, :, :], 2048, cast_eng="act")
                            wv = wb[:].rearrange("p (a b) -> p a b", a=16)
                        if it + 2 < len(its):
                            xload(it + 2)
                        i = it % 4
                        if tm == 0:
                            t0, mi = 0, 4
                        else:
                            t0, mi = TILES[tm][0], 0
                        xr = k.R("xres", c, tm)
                        ps, psr = bank()
                        mm_group(ps, psr, [wv[:, kc, :] for kc in range(16)], [acc[:, kc, m0:m0 + n] for kc in range(16)],
                                 [wbr] + [k.R("acc", kc, tm) for kc in range(16)], n)
                        tf, tfr = next_etf()
                        k.op("act", [psr, k.R("der", l)], [tfr], lambda e, ps=ps, tf=tf, c=c, mi=mi, n=n: e.activation(
                            out=tf[:, 0:n], in_=ps[:, 0:n], func=AF.Identity,
                            scale=der[:, l, mi + 2, c:c + 1], bias=der[:, l, mi + 3, c:c + 1]))
                        k.op("dve", [tfr, k.R("xb", i)], [k.R("xb", i)], lambda e, tf=tf, i=i, n=n: e.tensor_tensor(
                            out=xb[i][:, 0:n], in0=xb[i][:, 0:n], in1=tf[:, 0:n], op=ALU.add))
                        k.dma(xres[c, :, t0:t0 + n], xb[i][:, 0:n], [k.R("xb", i)], [xr])
                    k.barrier()

        def halo_exchange():
            allx = lambda tt: [k.R("xres", c, tt) for c in range(NCH)]
            hv = hxs.ap().rearrange("(c p) f -> c p f", p=128)
            k.dma(hv[:, :, 0:16], xres[:, :, P_OWN - 16:P_OWN], allx(1), [k.R("hxs")], engname="pool")
            k.dma(hv[:, :, 16:32], xres[:, :, T - 16:T], allx(4), [k.R("hxs")], engname="pool")
            k.coll([k.R("hxs")], [k.R("hxd")], lambda e: e.collective_compute(
                "AllGather", ALU.bypass, replica_groups=RG, ins=[hxs.ap().opt()], outs=[hxd.ap().opt()]))
            dv = hxd.ap().rearrange("(r c p) f -> r c p f", r=2, p=128)
            k.dma(xres[:, :, CTX:CTX + 16], dv[0, :, :, 16:32], [k.R("hxd")], allx(0), engname="pool")
            k.dma(xres[:, :, CTX + 16:CTX + 32], dv[1, :, :, 0:16], [k.R("hxd")], allx(0), engname="pool")

        assert TILES[1][0] == CTX + 2 * HALO

        for l in range(L):
            last = (l == L - 1)
            with contextlib.ExitStack() as st:
                layer_front(st, l, last)
                k.barrier()
            post_phase(l, last)
            if not last:
                halo_exchange()
        with contextlib.ExitStack() as st:
            norm_phase(st, L - 1, True, None)
        k.barrier()
    return nc


def _tile_w(w, nk, nn):
    return np.ascontiguousarray(w.reshape(nk, 128, nn, 128).transpose(2, 1, 0, 3).reshape(nn, 128, nk * 128))


def _vec(v, n):
    return np.ascontiguousarray(v.reshape(n, 128).T)


def _prep_shared(inp, L):
    f = np.float32
    sh = {}
    sh["w_ada"] = np.stack([_tile_w(inp["w_ada"][l], 16, 48) for l in range(L)])
    sh["w_in"] = np.stack([_tile_w(inp["w_in"][l], 16, NW_IN) for l in range(L)])
    sh["pwc_w"] = np.stack([_tile_w(inp["pwc_w"][l], 8, 8) for l in range(L)])
    sh["w_out"] = np.stack([_tile_w(inp["w_out"][l], 16, 16) for l in range(L)])
    sh["w_bout"] = np.stack([np.stack([_tile_w(inp["w_bout"][l][kb], 8, 16) for kb in range(3)]) for l in range(L)])
    pw = np.zeros((L, 8, 128, 256), f)
    for l in range(L):
        for jo in range(8):
            g, jj = jo // 2, jo % 2
            blk = inp["pool_w"][l][g][:, jj * 128:(jj + 1) * 128]
            pw[l, jo] = blk.reshape(2, 128, 128).transpose(1, 0, 2).reshape(128, 256)
    sh["pool_w"] = pw
    sh["ident"] = np.eye(128, dtype=f)
    rcl = np.zeros((4, 64), f)
    rcc = np.zeros((4, 256), f)
    for g, w in enumerate(POOL_W):
        for n_, tab in ((64, rcl), (256, rcc)):
            t = np.arange(n_)
            lo = np.maximum(t - w // 2, 0)
            hi = np.minimum(t + w // 2, n_)
            tab[g] = 1.0 / (hi - lo)
    sh["rcl"] = np.ascontiguousarray(np.broadcast_to(rcl.reshape(1, -1), (128, 256)))
    sh["rcc"] = np.ascontiguousarray(np.broadcast_to(rcc.reshape(1, -1), (128, 1024)))
    sh["final_g"] = _vec(inp["final_g"], 16)
    return sh


def _prep_parity(inp, L, par):
    f = np.float32
    wg = np.zeros((L, 8, 128, 512), f)
    sm = np.zeros((128, L * NSM), f)
    for l in range(L):
        for h in range(8):
            for d in range(2):
                wg[l, h, :, (d * 2 + 0) * 128:(d * 2 + 1) * 128] = inp["lru_wr"][l][d][h]
                wg[l, h, :, (d * 2 + 1) * 128:(d * 2 + 2) * 128] = inp["lru_wi"][l][d][h]
        def put(name, arr):
            o, s = SM[name]
            assert arr.shape == (128, s), (name, arr.shape)
            sm[:, l * NSM + o:l * NSM + o + s] = arr
        put("norm_g", _vec(inp["norm_g"][l], 16))
        put("b_out", _vec(inp["b_out"][l], 16))
        put("b_ada", _vec(inp["b_ada"][l], 48))
        cw = np.zeros((128, 8, 5), f)
        cw[:, :, 0:4] = inp["conv_a_w"][l].reshape(4, 8, 128).transpose(2, 1, 0)
        put("cwA", cw.reshape(128, 40))
        put("cbA", _vec(inp["conv_a_b"][l], 8))
        bg = np.zeros((128, 8, 4), f)
        for d in range(2):
            bg[:, :, d * 2 + 0] = inp["lru_br"][l][d].T
            bg[:, :, d * 2 + 1] = inp["lru_bi"][l][d].T
        put("bg", bg.reshape(128, 32))
        lam = np.zeros((128, 8, 2), f)
        for d in range(2):
            lam[:, :, d] = inp["lru_lambda"][l][d].reshape(8, 128).T
        put("lam", lam.reshape(128, 16))
        put("pool_scale", _vec(inp["pool_scale"][l], 8))
        put("cwC", np.ascontiguousarray(inp["convc_w"][l].reshape(31, 8, 128).transpose(2, 1, 0)).reshape(128, 248))
        put("cbC", _vec(inp["convc_b"][l], 8))
        put("lnc_g", _vec(inp["lnc_g"][l], 8))
        put("lnc_b", _vec(inp["lnc_b"][l], 8))
        put("pwc_b", _vec(inp["pwc_b"][l], 8))
    return {"wg": wg, "sm": sm}


def _prep_core(inp, c):
    f = np.float32
    b, half = c // 2, c % 2
    x = inp["x"][b]
    own = x[half * HALF:(half + 1) * HALF]
    hl = x[half * HALF - HALO:half * HALF] if half == 1 else np.zeros((HALO, D), f)
    hr = x[(half + 1) * HALF:(half + 1) * HALF + HALO] if half == 0 else np.zeros((HALO, D), f)
    tok = np.concatenate([inp["ctx"][b], hl, hr, own], axis=0)
    xs = np.ascontiguousarray(tok.T).reshape(NCH, 128, T)
    cv = np.zeros((128, 16, 2), f)
    cv[:, :, 0] = inp["c"][b].reshape(16, 128).T
    cv[:, :, 1] = inp["c_ctx"].reshape(16, 128).T
    e = 1.0 if half == 0 else 0.0
    fl = np.tile(np.array([e, 1 - e, 1 - e, e], f).reshape(1, 4), (128, 1))
    return {"xs": xs, "cvec": cv.reshape(128, 32), "flags": np.ascontiguousarray(fl)}


def run(inp, L=4):
    inp = {k_: np.asarray(v, dtype=np.float32) for k_, v in inp.items()}
    nc = build_program(L)
    shared = _prep_shared(inp, L)
    shared.update(_prep_parity(inp, L, 0))
    in_maps = []
    for c in range(8):
        m = dict(shared)
        m.update(_prep_core(inp, c))
        in_maps.append(m)
    res = run_bass_kernel_spmd(nc, in_maps, core_ids=list(range(8)))
    outp = np.zeros((4, SEQ, D), np.float32)
    for c in range(8):
        o = np.asarray(res.results[c]["out"]).reshape(D, HALF)
        outp[c // 2, (c % 2) * HALF:(c % 2 + 1) * HALF] = o.T
    return outp


def kernel(**inputs):
    return run(inputs, 4)
```

```python
import contextlib
import numpy as np
import concourse.bass as bass
import concourse.mybir as mybir
from concourse.bass_utils import run_bass_kernel_spmd

F32 = mybir.dt.float32
BF16 = mybir.dt.bfloat16
AF = mybir.ActivationFunctionType
ALU = mybir.AluOpType

D = 2048
NCH = 16
SEQ = 4096
HALF = 2048
CTX = 256
HALO = 16
T = CTX + 2 * HALO + HALF
TM = CTX + HALF
TILES = [(0, CTX + 2 * HALO)] + [(CTX + 2 * HALO + i * 512, 512) for i in range(4)]
TMT = [(0, CTX)] + [(CTX + i * 512, 512) for i in range(4)]
WP = 2384
P_CTX = 16
P_HL = 288
P_OWN = 304
P_HR = 2352
POOL_W = (2, 4, 8, 16)
EPS = 1e-6
NW_IN = 104

SM = {}
_o = 0
for _n, _s in [("norm_g", 16), ("b_out", 16), ("b_ada", 48), ("cwA", 40), ("cbA", 8), ("bg", 32), ("lam", 16),
               ("pool_scale", 8), ("cwC", 248), ("cbC", 8), ("lnc_g", 8), ("lnc_b", 8), ("pwc_b", 8)]:
    SM[_n] = (_o, _s)
    _o += _s
NSM = _o


class Res:
    __slots__ = ("last_w", "reads")

    def __init__(self):
        self.last_w = None
        self.reads = {}


class Eng:
    def __init__(self, name, e, sem):
        self.name = name
        self.e = e
        self.sem = sem
        self.cnt = 0
        self.seen = {}
        self.ring = []
        self.dma_idx = 0


class K:
    def __init__(self, nc, stack):
        self.nc = nc
        self.engs = {}
        for name, e in [("pe", nc.tensor), ("act", nc.scalar), ("dve", nc.vector), ("pool", nc.gpsimd), ("sp", nc.sync)]:
            sem = stack.enter_context(nc.semaphore("s_" + name))
            self.engs[name] = Eng(name, e, sem)
        for name, k in [("sp", 24), ("act", 4), ("pool", 8)]:
            for i in range(k):
                sem = stack.enter_context(nc.semaphore("d_%s%d" % (name, i)))
                self.engs[name].ring.append([sem, 0, "d_%s%d" % (name, i)])
        self.cc_sem = stack.enter_context(nc.semaphore("s_cc"))
        self.cc_cnt = 0
        self.resd = {}

    def R(self, *key):
        r = self.resd.get(key)
        if r is None:
            r = self.resd[key] = Res()
        return r

    @staticmethod
    def _deps(reads, writes):
        toks = {}

        def add(tok):
            if tok is None:
                return
            k = tok[2]
            if k not in toks or toks[k][1] < tok[1]:
                toks[k] = tok
        for r in reads:
            add(r.last_w)
        for w in writes:
            add(w.last_w)
            for tok in w.reads.values():
                add(tok)
        return list(toks.values())

    def _wait(self, eng, toks):
        for tok in toks:
            if tok[2] == eng.name and eng.name == "pe":
                continue
            if eng.seen.get(tok[2], 0) < tok[1]:
                eng.e.wait_ge(tok[0], tok[1])
                eng.seen[tok[2]] = tok[1]

    @staticmethod
    def _mark(tok, reads, writes):
        for r in reads:
            r.reads[tok[2]] = tok
        for w in writes:
            w.last_w = tok
            w.reads = {}

    def op(self, engname, reads, writes, emit):
        eng = self.engs[engname]
        self._wait(eng, self._deps(reads, writes))
        inst = emit(eng.e)
        eng.cnt += 1
        inst.then_inc(eng.sem, 1)
        self._mark((eng.sem, eng.cnt, eng.name), reads, writes)

    def dma(self, out_ap, in_ap, reads, writes, engname="sp"):
        eng = self.engs[engname]
        slot = eng.ring[eng.dma_idx % len(eng.ring)]
        eng.dma_idx += 1
        sem, n, key = slot
        toks = self._deps(reads, writes)
        if n > 0:
            toks.append((sem, 16 * n, key))
        self._wait(eng, toks)
        eng.e.dma_start(out=out_ap, in_=in_ap).then_inc(sem, 16)
        slot[1] = n + 1
        self._mark((sem, 16 * (n + 1), key), reads, writes)

    def coll(self, reads, writes, emit):
        eng = self.engs["pool"]
        self._wait(eng, self._deps(reads, writes))
        inst = emit(eng.e)
        self.cc_cnt += 1
        inst.then_inc(self.cc_sem, 1)
        self._mark((self.cc_sem, self.cc_cnt, "cc"), reads, writes)

    def barrier(self):
        toks = []
        for e in self.engs.values():
            if e.cnt > 0:
                toks.append((e.sem, e.cnt, e.name))
            for sem, n, key in e.ring:
                if n > 0:
                    toks.append((sem, 16 * n, key))
        if self.cc_cnt:
            toks.append((self.cc_sem, self.cc_cnt, "cc"))
        for e in self.engs.values():
            for tok in toks:
                if tok[2] == e.name:
                    continue
                if e.seen.get(tok[2], 0) < tok[1]:
                    e.e.wait_ge(tok[0], tok[1])
                    e.seen[tok[2]] = tok[1]


def build_program(L=4):
    nc = bass.Bass("TRN2", target_bir_lowering=False)
    dt = nc.dram_tensor
    xs = dt("xs", [NCH, 128, T], F32, kind="ExternalInput")
    cvec = dt("cvec", [128, 32], F32, kind="ExternalInput")
    flags = dt("flags", [128, 4], F32, kind="ExternalInput")
    ident_d = dt("ident", [128, 128], F32, kind="ExternalInput")
    rcl_d = dt("rcl", [128, 4 * 64], F32, kind="ExternalInput")
    rcc_d = dt("rcc", [128, 4 * 256], F32, kind="ExternalInput")
    sm_d = dt("sm", [128, L * NSM], F32, kind="ExternalInput")
    fg_d = dt("final_g", [128, 16], F32, kind="ExternalInput")
    w_ada = dt("w_ada", [L, 48, 128, 2048], F32, kind="ExternalInput")
    w_in = dt("w_in", [L, NW_IN, 128, 2048], F32, kind="ExternalInput")
    wg_d = dt("wg", [L, 8, 128, 512], F32, kind="ExternalInput")
    pool_w = dt("pool_w", [L, 8, 128, 256], F32, kind="ExternalInput")
    pwc_w = dt("pwc_w", [L, 8, 128, 1024], F32, kind="ExternalInput")
    w_bout = dt("w_bout", [L, 3, 16, 128, 1024], F32, kind="ExternalInput")
    w_out = dt("w_out", [L, 16, 128, 2048], F32, kind="ExternalInput")
    out = dt("out", [NCH, 128, HALF], F32, kind="ExternalOutput")
    xres = dt("xres", [NCH, 128, T], F32)
    zs = dt("zs", [24, 128, T], BF16)
    gs = dt("gs", [48, 128, T], BF16)
    ys = dt("ys", [24, 128, TM], BF16)
    ycv = dt("ycv", [8, 128, TM], BF16)
    yloc = dt("yloc", [8, 128, HALF], F32)
    afd = dt("afd", [8, 128, HALF], F32)
    abd = dt("abd", [8, 128, HALF], F32)
    cc1s = dt("cc1s", [128, 16], F32)
    cc1d = dt("cc1d", [256, 16], F32)
    hxs = dt("hxs", [2048, 32], F32)
    hxd = dt("hxd", [4096, 32], F32)
    RG = [[0, 1], [2, 3], [4, 5], [6, 7]]

    with contextlib.ExitStack() as stack:
        k = K(nc, stack)
        uid = [0]

        def sb(name, shape, dtype, st=stack):
            uid[0] += 1
            return st.enter_context(nc.sbuf_tensor("sb%d_%s" % (uid[0], name), shape, dtype))
        banks = [stack.enter_context(nc.psum_tensor("ps%d" % i, [128, 512], F32)) for i in range(8)]
        bank_i = [0]

        reserved = set()

        def bank(reserve=False):
            while True:
                i = bank_i[0] % 8
                bank_i[0] += 1
                if i not in reserved:
                    break
            if reserve:
                reserved.add(i)
            return banks[i], k.R("ps", i)

        def unreserve(ps):
            for i in range(8):
                if banks[i] is ps:
                    reserved.discard(i)

        sm = sb("sm", [128, L * NSM], F32)
        cv_sb = sb("cvec", [128, 32], F32)
        scv = sb("scv", [128, 32], BF16)
        fl = sb("flags", [128, 4], F32)
        ident = sb("ident", [128, 128], F32)
        fg = sb("fg", [128, 16], F32)
        ones_d = sb("ones_d", [128, 128], BF16)
        ones_c = sb("ones_c", [128, 128], BF16)
        zcol = sb("zcol", [128, 1], F32)
        modsb = sb("modsb", [128, L, 48, 2], F32)
        der = sb("der", [128, L, 8, 16], F32)
        nls = sb("nls", [128, L, 2, 16], F32)
        st_c = sb("st_c", [128, 16], F32)
        st_o = sb("st_o", [128, 16], F32)
        st_g = sb("st_g", [128, 2, 16], F32)
        st_i = sb("st_i", [128, 16], F32)
        st_t = sb("st_t", [128, 16], F32)
        wst = [sb("wst%d" % i, [128, 2048], F32) for i in range(2)]
        wbf = [sb("wbf%d" % i, [128, 2048], BF16) for i in range(2)]
        etb = [sb("etb%d" % i, [128, 512], BF16) for i in range(3)]
        etf = [sb("etf%d" % i, [128, 512], F32) for i in range(3)]
        cnt = {"w": 0, "etb": 0, "etf": 0}

        def smv(l, name, i=None, n=1):
            o, s = SM[name]
            base = l * NSM + o
            if i is None:
                return sm[:, base:base + s]
            return sm[:, base + i:base + i + n]

        def next_etb():
            i = cnt["etb"] % 3
            cnt["etb"] += 1
            return etb[i], k.R("etb", i)

        def next_etf():
            i = cnt["etf"] % 3
            cnt["etf"] += 1
            return etf[i], k.R("etf", i)

        def load_w(src_ap, ncols, cast=True, cast_eng=None):
            i = cnt["w"] % 2
            cnt["w"] += 1
            k.dma(wst[i][:, 0:ncols], src_ap, [], [k.R("wst", i)])
            if cast_eng is None:
                cast_eng = "act"
            if not cast:
                return wst[i], k.R("wst", i)
            if cast_eng == "act":
                k.op("act", [k.R("wst", i)], [k.R("wbf", i)],
                     lambda e: e.activation(out=wbf[i][:, 0:ncols], in_=wst[i][:, 0:ncols], func=AF.Identity))
            else:
                k.op(cast_eng, [k.R("wst", i)], [k.R("wbf", i)],
                     lambda e: e.tensor_copy(out=wbf[i][:, 0:ncols], in_=wst[i][:, 0:ncols]))
            return wbf[i], k.R("wbf", i)

        def mm_group(ps, psr, lhs_list, rhs_list, reads, n):
            def emit(e):
                inst = None
                nk = len(lhs_list)
                for i in range(nk):
                    inst = e.matmul(ps[:, 0:n], lhsT=lhs_list[i], rhs=rhs_list[i], start=(i == 0), stop=(i == nk - 1))
                return inst
            k.op("pe", reads, [psr], emit)

        k.dma(sm[:], sm_d[:, :], [], [k.R("sm")])
        k.dma(cv_sb[:], cvec[:, :], [], [k.R("cvec")])
        k.dma(fl[:], flags[:, :], [], [k.R("fl")])
        k.dma(ident[:], ident_d[:, :], [], [k.R("ident")])
        k.dma(fg[:], fg_d[:, :], [], [k.R("fg")])
        k.op("dve", [], [k.R("ones_d")], lambda e: e.memset(ones_d[:], 1.0 / D))
        k.op("dve", [], [k.R("ones_c")], lambda e: e.memset(ones_c[:], 1.0 / 1024))
        k.op("dve", [], [k.R("zcol")], lambda e: e.memset(zcol[:], 0.0))
        k.op("act", [k.R("cvec")], [k.R("scv")], lambda e: e.activation(out=scv[:], in_=cv_sb[:], func=AF.Silu))
        mod_ps = {}

        def mod_slice(l, sl, nper=6, n0=None, n1=None):
            if n0 is None:
                n0, n1 = sl * nper, sl * nper + nper
            if n0 == 0:
                mod_ps[l] = bank(reserve=True)
            ps, psr = mod_ps[l]
            for n in range(n0, n1):
                wa, war = load_w(w_ada[l, n, :, :], 2048, cast_eng=("act" if n % 2 == 0 else "dve"))
                wa3 = wa[:].rearrange("p (a b) -> p a b", a=16)

                def emit(e, n=n, wa3=wa3, ps=ps):
                    inst = None
                    for kc in range(16):
                        inst = e.matmul(ps[:, 2 * n:2 * n + 2], lhsT=wa3[:, kc, :], rhs=scv[:, 2 * kc:2 * kc + 2],
                                        start=(kc == 0), stop=(kc == 15))
                    return inst
                k.op("pe", [war, k.R("scv")], [psr], emit)
            if n1 == 48:
                mod_finish(l)

        def mod_finish(l):
            ps, psr = mod_ps[l]
            unreserve(ps)
            ba = smv(l, "b_ada")
            k.op("dve", [psr, k.R("sm")], [k.R("modsb", l)],
                 lambda e, l=l, ps=ps, ba=ba: e.tensor_tensor(
                     out=modsb[:, l, :, :], in0=ps[:, 0:96].rearrange("p (a b) -> p a b", b=2),
                     in1=ba.unsqueeze(2).to_broadcast([128, 48, 2]), op=ALU.add))
            for m in range(2):
                shift = modsb[:, l, 0:16, m]
                scale = modsb[:, l, 16:32, m]
                gate = modsb[:, l, 32:48, m]
                rr = [k.R("modsb", l), k.R("sm")]
                ww = [k.R("der", l)]
                k.op("dve", rr, ww, lambda e, l=l, m=m, scale=scale: e.scalar_tensor_tensor(
                    out=der[:, l, 4 * m + 0, :], in0=scale, scalar=1.0, in1=smv(l, "norm_g"), op0=ALU.add, op1=ALU.mult))
                k.op("dve", rr, ww, lambda e, l=l, m=m, shift=shift: e.tensor_copy(out=der[:, l, 4 * m + 1, :], in_=shift))
                k.op("dve", rr, ww, lambda e, l=l, m=m, gate=gate: e.tensor_copy(out=der[:, l, 4 * m + 2, :], in_=gate))
                k.op("dve", rr, ww, lambda e, l=l, m=m, gate=gate: e.tensor_tensor(
                    out=der[:, l, 4 * m + 3, :], in0=gate, in1=smv(l, "b_out"), op=ALU.mult))

        for l in range(L):
            k.op("act", [k.R("sm")], [k.R("nls", l)],
                 lambda e, l=l: e.activation(out=nls[:, l, 0, :], in_=smv(l, "lam"), func=AF.Exp, scale=-1.0))
            k.op("act", [k.R("nls", l)], [k.R("nls", l)],
                 lambda e, l=l: e.activation(out=nls[:, l, 0, :], in_=nls[:, l, 0, :], func=AF.Ln, bias=1.0))
            k.op("dve", [k.R("nls", l)], [k.R("nls", l)],
                 lambda e, l=l: e.tensor_scalar(out=nls[:, l, 1, :], in0=nls[:, l, 0, :], scalar1=-16.0, scalar2=None, op0=ALU.mult))
            k.op("dve", [k.R("nls", l)], [k.R("nls", l)],
                 lambda e, l=l: e.tensor_scalar(out=nls[:, l, 0, :], in0=nls[:, l, 0, :], scalar1=-8.0, scalar2=None, op0=ALU.mult))
        for sl in range(8):
            mod_slice(0, sl)

        def norm_phase(st, l, final, hT):
            xt = [sb("xt%d" % i, [128, 16, 512], F32, st) for i in range(2)]
            sq = [sb("sq%d" % i, [128, 512], BF16, st) for i in range(2)]
            rs = [sb("rs%d" % i, [128, 512], F32, st) for i in range(2)]
            tiles = list(enumerate(TILES))
            if final:
                tiles = tiles[1:]
            else:
                tiles = tiles[1:] + tiles[:1]
            def xt_load(it_):
                tt_, (t0_, n_) = tiles[it_]
                from_in = (l == 0 and not final)
                src_ = xs if from_in else xres
                k.dma(xt[it_ % 2][:, :, 0:n_], src_[:, :, t0_:t0_ + n_].rearrange("c p t -> p c t"),
                      [] if from_in else [k.R("xres", c, tt_) for c in range(NCH)], [k.R("xt", it_ % 2)])
            xt_load(0)
            for it, (tt, (t0, n)) in enumerate(tiles):
                b = it % 2
                xtr = k.R("xt", b)
                ps, psr = bank()
                for c in range(NCH):
                    sqr = k.R("sq", c % 2)
                    k.op("act", [xtr], [sqr], lambda e, c=c, b=b, n=n: e.activation(
                        out=sq[c % 2][:, 0:n], in_=xt[b][:, c, 0:n], func=AF.Square))
                    k.op("pe", [sqr, k.R("ones_d")], [psr], lambda e, c=c, ps=ps, n=n: e.matmul(
                        ps[:, 0:n], lhsT=ones_d[:], rhs=sq[c % 2][:, 0:n], start=(c == 0), stop=(c == NCH - 1)))
                if it + 1 < len(tiles):
                    xt_load(it + 1)
                rsr = k.R("rs", b)
                k.op("act", [psr], [rsr], lambda e, b=b, ps=ps, n=n: e.activation(
                    out=rs[b][:, 0:n], in_=ps[:, 0:n], func=AF.Sqrt, bias=EPS))
                k.op("dve", [rsr], [rsr], lambda e, b=b, n=n: e.reciprocal(out=rs[b][:, 0:n], in_=rs[b][:, 0:n]))
                for c in range(NCH):
                    if (not final) and l + 1 < L and c < 7 and it * 7 + c < 32:
                        mod_slice(l + 1, None, n0=it * 7 + c, n1=it * 7 + c + 1)
                    tf, tfr = next_etf()
                    k.op("dve", [xtr, rsr], [tfr], lambda e, c=c, b=b, n=n, tf=tf: e.tensor_tensor(
                        out=tf[:, 0:n], in0=xt[b][:, c, 0:n], in1=rs[b][:, 0:n], op=ALU.mult))
                    if final:
                        of, ofr = next_etf()
                        k.op("act", [tfr, k.R("fg")], [ofr], lambda e, c=c, n=n, tf=tf, of=of: e.activation(
                            out=of[:, 0:n], in_=tf[:, 0:n], func=AF.Identity, scale=fg[:, c:c + 1]))
                        k.dma(out[c, :, (tt - 1) * 512:(tt - 1) * 512 + n], of[:, 0:n], [ofr], [k.R("out", c, tt)])
                    else:
                        hr = k.R("hT", tt)
                        dr = k.R("der", l)
                        if tt == 0:
                            k.op("act", [tfr, dr], [hr], lambda e, c=c, tf=tf: e.activation(
                                out=hT[:, c, 0:CTX], in_=tf[:, 0:CTX], func=AF.Identity,
                                scale=der[:, l, 4, c:c + 1], bias=der[:, l, 5, c:c + 1]))
                            k.op("act", [tfr, dr], [hr], lambda e, c=c, tf=tf: e.activation(
                                out=hT[:, c, CTX:CTX + 32], in_=tf[:, CTX:CTX + 32], func=AF.Identity,
                                scale=der[:, l, 0, c:c + 1], bias=der[:, l, 1, c:c + 1]))
                        else:
                            k.op("act", [tfr, dr], [hr], lambda e, c=c, tf=tf, t0=t0, n=n: e.activation(
                                out=hT[:, c, t0:t0 + n], in_=tf[:, 0:n], func=AF.Identity,
                                scale=der[:, l, 0, c:c + 1], bias=der[:, l, 1, c:c + 1]))

        def layer_front(st, l, last):
            hT = sb("hT", [128, NCH, T], BF16, st)
            with contextlib.ExitStack() as st2:
                norm_phase(st2, l, False, hT)
                k.barrier()
            rows = [sb("row%d" % i, [128, WP], F32, st) for i in range(4)]
            rr_ = [k.R("row", i) for i in range(4)]
            uab = sb("uab", [128, WP], BF16, st)
            glu = sb("glu", [128, WP], BF16, st)
            prow = [sb("prow%d" % i, [128, TM], BF16, st) for i in range(2)]
            wgst = sb("wgst", [128, 512], F32, st)
            wgb = sb("wgb", [128, 512], BF16, st)
            dg = sb("dg", [128, 31, 128], BF16, st)
            pb = [sb("pb%d" % i, [128, 800], F32, st) for i in range(3)]
            pbc = sb("pbc", [128, 304], F32, st)
            pwst = sb("pwst", [128, 256], F32, st)
            yctx_t = sb("yctx_t", [128, CTX], BF16, st)
            rcl = sb("rcl", [128, 4, 64], F32, st)
            rcc = sb("rcc", [128, 4, 256], F32, st)
            k.dma(rcl[:].rearrange("p a b -> p (a b)"), rcl_d[:, :], [], [k.R("rcl")])
            k.dma(rcc[:].rearrange("p a b -> p (a b)"), rcc_d[:, :], [], [k.R("rcc")])
            pwb = [sb("pwb%d" % i, [128, 256], BF16, st) for i in range(2)]
            for i in range(4):
                k.op("pool", [], [rr_[i]], lambda e, i=i: e.memset(rows[i][:], 0.0))
            k.op("pool", [], [k.R("glu")], lambda e: e.memset(glu[:], 0.0))
            k.op("pool", [], [k.R("uab")], lambda e: e.memset(uab[:], 0.0))
            for i in range(3):
                k.op("pool", [], [k.R("pb", i)], lambda e, i=i: e.memset(pb[i][:], 0.0))
            k.op("pool", [], [k.R("pbc")], lambda e: e.memset(pbc[:], 0.0))

            hTr = [k.R("hT", tt) for tt in range(5)]

            def win_matmul(n_idx, evac, skip0=False):
                wb, wbr = pending.pop(0)
                wb3 = wb[:].rearrange("p (a b) -> p a b", a=16)
                for tt, (t0, n) in enumerate(TILES):
                    if skip0 and tt == 0:
                        continue
                    ps, psr = bank()
                    mm_group(ps, psr, [wb3[:, kc, :] for kc in range(16)], [hT[:, kc, t0:t0 + n] for kc in range(16)],
                             [wbr, hTr[tt]], n)
                    evac(tt, t0, n, ps, psr)

            pending = []

            def prefetch(n_idx):
                pending.append(load_w(w_in[l, n_idx, :, :], 2048))

            def evac_store(func, dst, dname, di):
                def f(tt, t0, n, ps, psr):
                    tb, tbr = next_etb()
                    k.op("act", [psr], [tbr], lambda e: e.activation(out=tb[:, 0:n], in_=ps[:, 0:n], func=func))
                    k.dma(dst[di, :, t0:t0 + n], tb[:, 0:n], [tbr], [k.R(dname, di, tt)])
                return f

            def evac_row_padded(row, rowr, dtype_is_bf=False):
                def f(tt, t0, n, ps, psr):
                    if tt == 0:
                        k.op("act", [psr], [rowr], lambda e: e.activation(
                            out=row[:, P_CTX:P_CTX + CTX], in_=ps[:, 0:CTX], func=AF.Identity))
                        k.op("dve", [psr, k.R("fl")], [rowr], lambda e: e.tensor_scalar(
                            out=row[:, P_HL:P_HL + 16], in0=ps[:, CTX:CTX + 16], scalar1=fl[:, 2:3], scalar2=None, op0=ALU.mult))
                        k.op("dve", [psr, k.R("fl")], [rowr], lambda e: e.tensor_scalar(
                            out=row[:, P_HR:P_HR + 16], in0=ps[:, CTX + 16:CTX + 32], scalar1=fl[:, 3:4], scalar2=None, op0=ALU.mult))
                    else:
                        o = P_OWN + (tt - 1) * 512
                        k.op("act", [psr], [rowr], lambda e: e.activation(out=row[:, o:o + 512], in_=ps[:, 0:512], func=AF.Identity))
                return f

            X_, R_, I_, E_ = range(4)
            lo, hi = P_CTX, P_OWN + HALF
            c_lo, c_hi = P_CTX, P_CTX + CTX
            o_lo, o_hi = P_OWN, P_OWN + HALF
            gate_tiles = [(P_CTX, CTX)] + [(P_OWN + i * 512, 512) for i in range(4)]

            def a_dir(h, d, S_):
                smr = k.R("sm")
                nr = k.R("nls", l)
                zb_ = zcol[:, 0:1].to_broadcast([128, HALF])
                for (c0, n) in gate_tiles:
                    for g2 in range(2):
                        q = d * 2 + g2
                        ps, psr = bank()
                        mm_group(ps, psr, [wgb[:, q * 128:(q + 1) * 128]], [uab[:, c0:c0 + n]], [k.R("wgb"), k.R("uab")], n)
                        rq = (R_, I_)[g2]
                        k.op("act", [psr, smr], [rr_[rq]], lambda e, q=q, rq=rq, ps=ps, c0=c0, n=n: e.activation(
                            out=rows[rq][:, c0:c0 + n], in_=ps[:, 0:n], func=AF.Sigmoid, bias=smv(l, "bg", h * 4 + q)))
                s1 = nls[:, l, 0, h * 2 + d:h * 2 + d + 1]
                s2 = nls[:, l, 1, h * 2 + d:h * 2 + d + 1]
                for (a, b) in ((c_lo, c_hi), (o_lo, o_hi)):
                    k.op("act", [rr_[R_], nr], [rr_[S_]], lambda e, a=a, b=b: e.activation(
                        out=rows[S_][:, a:b], in_=rows[R_][:, a:b], func=AF.Exp, scale=s2))
                    k.op("act", [rr_[R_], nr], [rr_[R_]], lambda e, a=a, b=b: e.activation(
                        out=rows[R_][:, a:b], in_=rows[R_][:, a:b], func=AF.Exp, scale=s1))
                    k.op("act", [rr_[S_]], [rr_[S_]], lambda e, a=a, b=b: e.activation(
                        out=rows[S_][:, a:b], in_=rows[S_][:, a:b], func=AF.Sqrt, scale=-1.0, bias=1.0))
                    k.op("dve", [rr_[I_], rr_[S_]], [rr_[I_]], lambda e, a=a, b=b: e.tensor_tensor(
                        out=rows[I_][:, a:b], in0=rows[I_][:, a:b], in1=rows[S_][:, a:b], op=ALU.mult))
                    k.op("dve", [rr_[I_], k.R("uab")], [rr_[I_]], lambda e, a=a, b=b: e.tensor_tensor(
                        out=rows[I_][:, a:b], in0=rows[I_][:, a:b], in1=uab[:, a:b], op=ALU.mult))
                if d == 0:
                    v = lambda r, a, b: rows[r][:, a:b]
                else:
                    v = lambda r, a, b: rows[r][:, a:b][:, ::-1]
                k.op("dve", [rr_[R_], rr_[I_]], [rr_[S_]], lambda e: e.tensor_tensor_scan(
                    out=v(S_, c_lo, c_hi), data0=v(R_, c_lo, c_hi), data1=v(I_, c_lo, c_hi), initial=0.0, op0=ALU.mult, op1=ALU.add))
                k.op("dve", [rr_[R_], rr_[I_]], [rr_[S_]], lambda e: e.tensor_tensor_scan(
                    out=v(S_, o_lo, o_hi), data0=v(R_, o_lo, o_hi), data1=v(I_, o_lo, o_hi), initial=0.0, op0=ALU.mult, op1=ALU.add))
                k.op("dve", [rr_[R_], k.R("zcol")], [rr_[I_]], lambda e: e.tensor_tensor_scan(
                    out=v(I_, o_lo, o_hi), data0=v(R_, o_lo, o_hi), data1=zb_, initial=1.0, op0=ALU.mult, op1=ALU.add))
                cpos = (c_hi - 1) if d == 0 else c_lo
                opos = (o_hi - 1) if d == 0 else o_lo
                si = d * 8 + h
                k.op("dve", [rr_[S_]], [k.R("st_c")], lambda e: e.tensor_copy(out=st_c[:, si:si + 1], in_=rows[S_][:, cpos:cpos + 1]))
                k.op("dve", [rr_[S_], rr_[I_], k.R("st_c")], [k.R("st_o")], lambda e: e.scalar_tensor_tensor(
                    out=st_o[:, si:si + 1], in0=rows[I_][:, opos:opos + 1], scalar=st_c[:, si:si + 1], in1=rows[S_][:, opos:opos + 1],
                    op0=ALU.mult, op1=ALU.add))
                dst = afd if d == 0 else abd
                k.dma(dst[h, :, :], rows[I_][:, o_lo:o_hi], [rr_[I_]], [k.R("afd" if d == 0 else "abd", h)], engname="pool")

            def mixer_a1(h):
                smr = k.R("sm")
                k.dma(wgst[:], wg_d[l, h, :, :], [], [k.R("wgst")])
                k.op("act", [k.R("wgst")], [k.R("wgb")], lambda e: e.activation(out=wgb[:], in_=wgst[:], func=AF.Identity))
                win_matmul(h, evac_row_padded(rows[X_], rr_[X_]))
                cw = lambda m: smv(l, "cwA", h * 5 + m)
                k.op("dve", [rr_[X_], smr], [rr_[E_]], lambda e: e.tensor_scalar(
                    out=rows[E_][:, lo:hi], in0=rows[X_][:, lo - 2:hi - 2], scalar1=cw(0), scalar2=smv(l, "cbA", h),
                    op0=ALU.mult, op1=ALU.add))
                for m in range(1, 5):
                    k.op("dve", [rr_[X_], rr_[E_], smr], [rr_[E_]], lambda e, m=m: e.scalar_tensor_tensor(
                        out=rows[E_][:, lo:hi], in0=rows[X_][:, lo - 2 + m:hi - 2 + m], scalar=cw(m), in1=rows[E_][:, lo:hi],
                        op0=ALU.mult, op1=ALU.add))
                k.op("dve", [rr_[E_]], [k.R("uab")], lambda e: e.tensor_copy(out=uab[:, lo:hi], in_=rows[E_][:, lo:hi]))

            def mixer_a1b(h):
                a_dir(h, 0, E_)

            def mixer_a2(h):
                a_dir(h, 1, X_)
                tb, tbr = yctx_t, k.R("yctx_t")
                k.op("dve", [rr_[E_], rr_[X_]], [tbr], lambda e: e.tensor_tensor(
                    out=tb[:, 0:CTX], in0=rows[E_][:, c_lo:c_hi], in1=rows[X_][:, c_lo:c_hi], op=ALU.add))
                k.dma(ys[h, :, 0:CTX], tb[:, 0:CTX], [tbr], [k.R("ys", h, 0)], engname="pool")
                k.op("dve", [rr_[E_], rr_[X_]], [rr_[E_]], lambda e: e.tensor_tensor(
                    out=rows[E_][:, o_lo:o_hi], in0=rows[E_][:, o_lo:o_hi], in1=rows[X_][:, o_lo:o_hi], op=ALU.add))
                k.dma(yloc[h, :, :], rows[E_][:, o_lo:o_hi], [rr_[E_]], [k.R("yloc", h)], engname="pool")

            def mixer_c(j):
                smr = k.R("sm")
                for tap in range(31):
                    k.op("act", [k.R("ident"), smr], [k.R("dg")], lambda e, tap=tap: e.activation(
                        out=dg[:, tap, :], in_=ident[:], func=AF.Identity, scale=smv(l, "cwC", j * 31 + tap)))
                cvt = {}

                def evac_cv(tt, t0, n, ps, psr):
                    tf, tfr = next_etf()
                    k.op("act", [psr], [tfr], lambda e: e.activation(out=tf[:, 0:n], in_=ps[:, 0:n], func=AF.Identity))
                    cvt[tt] = (tf, tfr)

                def evac_cg(tt, t0, n, ps, psr):
                    gr = k.R("glu")
                    if tt == 0:
                        k.op("act", [psr], [gr], lambda e: e.activation(out=glu[:, P_CTX:P_CTX + CTX], in_=ps[:, 0:CTX], func=AF.Sigmoid))
                        k.op("act", [psr], [gr], lambda e: e.activation(out=glu[:, P_HL:P_HL + 16], in_=ps[:, CTX:CTX + 16], func=AF.Sigmoid))
                        k.op("act", [psr], [gr], lambda e: e.activation(out=glu[:, P_HR:P_HR + 16], in_=ps[:, CTX + 16:CTX + 32], func=AF.Sigmoid))
                    else:
                        o = P_OWN + (tt - 1) * 512
                        k.op("act", [psr], [gr], lambda e: e.activation(out=glu[:, o:o + 512], in_=ps[:, 0:512], func=AF.Sigmoid))

                def evac_cvmul(tt, t0, n, ps, psr):
                    gr = k.R("glu")
                    fr = k.R("fl")
                    if tt == 0:
                        k.op("dve", [psr, gr], [gr], lambda e: e.tensor_tensor(
                            out=glu[:, P_CTX:P_CTX + CTX], in0=ps[:, 0:CTX], in1=glu[:, P_CTX:P_CTX + CTX], op=ALU.mult))
                        k.op("dve", [psr, gr, fr], [gr], lambda e: e.scalar_tensor_tensor(
                            out=glu[:, P_HL:P_HL + 16], in0=ps[:, CTX:CTX + 16], scalar=fl[:, 2:3], in1=glu[:, P_HL:P_HL + 16],
                            op0=ALU.mult, op1=ALU.mult))
                        k.op("dve", [psr, gr, fr], [gr], lambda e: e.scalar_tensor_tensor(
                            out=glu[:, P_HR:P_HR + 16], in0=ps[:, CTX + 16:CTX + 32], scalar=fl[:, 3:4], in1=glu[:, P_HR:P_HR + 16],
                            op0=ALU.mult, op1=ALU.mult))
                    else:
                        o = P_OWN + (tt - 1) * 512
                        k.op("dve", [psr, gr], [gr], lambda e: e.tensor_tensor(
                            out=glu[:, o:o + 512], in0=ps[:, 0:512], in1=glu[:, o:o + 512], op=ALU.mult))

                win_matmul(40 + j, evac_cg)
                win_matmul(32 + j, evac_cvmul)
                for tm, (c0, n) in enumerate([(P_CTX, CTX)] + [(P_OWN + i * 512, 512) for i in range(4)]):
                    ps, psr = bank()
                    mm_group(ps, psr, [dg[:, tap, :] for tap in range(31)],
                             [glu[:, c0 + tap - 15:c0 + tap - 15 + n] for tap in range(31)], [k.R("dg"), k.R("glu")], n)
                    tf, tfr = next_etb()
                    k.op("act", [psr, smr], [tfr], lambda e, ps=ps, n=n, tf=tf: e.activation(
                        out=tf[:, 0:n], in_=ps[:, 0:n], func=AF.Identity, bias=smv(l, "cbC", j)))
                    m0 = TMT[tm][0]
                    k.dma(ycv[j, :, m0:m0 + n], tf[:, 0:n], [tfr], [k.R("ycv", j, tm)])

            def mixer_b(j):
                g = j // 2
                w = POOL_W[g]
                nst = g + 1
                k.dma(pwst[:], pool_w[l, j, :, :], [], [k.R("pwst")])
                k.op("act", [k.R("pwst")], [k.R("pwb", j % 2)], lambda e: e.activation(out=pwb[j % 2][:], in_=pwst[:], func=AF.Identity))

                def evac_pool(tt, t0, n, ps, psr):
                    X, A, B = (pbc if tt == 0 else pb[0]), pb[1], pb[2]
                    xr, ar, br = (k.R("pbc") if tt == 0 else k.R("pb", 0)), k.R("pb", 1), k.R("pb", 2)
                    if tt == 0:
                        nrow, rl, base = 1, CTX, 16
                        span = 288
                    else:
                        nrow, rl, base = 8, 64, 16
                        span = 768
                    rowlen = rl + 32

                    def v(buf, off=0):
                        return buf[:, base + off:base + off + nrow * rowlen].rearrange("p (r c) -> p r c", c=rowlen)[:, :, 0:rl]
                    k.op("dve", [psr], [xr], lambda e: e.tensor_copy(
                        out=v(X), in_=ps[:, 0:nrow * rl].rearrange("p (r c) -> p r c", c=rl)))
                    lo_, hi_ = 8, span - 8
                    k.op("dve", [xr], [ar], lambda e: e.tensor_tensor(
                        out=A[:, lo_:hi_], in0=X[:, lo_ - 1:hi_ - 1], in1=X[:, lo_:hi_], op=ALU.add))
                    cur, curr, oth, othr = A, ar, B, br
                    sh = 1
                    for s in range(1, nst):
                        k.op("dve", [curr], [othr], lambda e, cur=cur, oth=oth, sh=sh: e.tensor_tensor(
                            out=oth[:, lo_:hi_], in0=cur[:, lo_ - sh:hi_ - sh], in1=cur[:, lo_ + sh:hi_ + sh], op=ALU.add))
                        cur, curr, oth, othr = oth, othr, cur, curr
                        sh *= 2
                    rc = (rcc[:, g, :].unsqueeze(1) if tt == 0 else rcl[:, g, :].unsqueeze(1).to_broadcast([128, 8, 64]))
                    k.op("dve", [curr, k.R("rcl"), k.R("rcc")], [othr], lambda e, cur=cur, oth=oth: e.tensor_tensor(
                        out=v(oth), in0=v(cur), in1=rc, op=ALU.mult))
                    pr = k.R("prow", j % 2)
                    m0 = TMT[tt][0]
                    k.op("dve", [othr, xr], [pr], lambda e, oth=oth: e.tensor_tensor(
                        out=prow[j % 2][:, m0:m0 + nrow * rl].rearrange("p (r c) -> p r c", c=rl), in0=v(oth), in1=v(X), op=ALU.subtract))

                def evac_pool_t0(tt, t0, n, ps, psr):
                    evac_pool(tt, t0, n, ps, psr)
                win_matmul(16 + j, evac_pool, skip0=last)
                if j % 2 == 1:
                    for jo in (j - 1, j):
                        wv = pwb[jo % 2][:].rearrange("p (a b) -> p a b", a=2)
                        for tm, (m0, n) in enumerate(TMT):
                            if last and tm == 0:
                                continue
                            ps, psr = bank()
                            mm_group(ps, psr, [wv[:, 0, :], wv[:, 1, :]], [prow[0][:, m0:m0 + n], prow[1][:, m0:m0 + n]],
                                     [k.R("pwb", jo % 2), k.R("prow", 0), k.R("prow", 1)], n)
                            tb, tbr = next_etb()
                            k.op("act", [psr, k.R("sm")], [tbr], lambda e, ps=ps, n=n, tb=tb, jo=jo: e.activation(
                                out=tb[:, 0:n], in_=ps[:, 0:n], func=AF.Identity, scale=smv(l, "pool_scale", jo)))
                            k.dma(ys[8 + jo, :, m0:m0 + n], tb[:, 0:n], [tbr], [k.R("ys", 8 + jo, tm)])

            jobs = []
            for j in range(8):
                jobs.append(("A1", j, [j]))
                for q in range(0, 2):
                    jobs.append(("G", j * 6 + q, [56 + j * 6 + q]))
                jobs.append(("A1b", j, []))
                for q in range(2, 4):
                    jobs.append(("G", j * 6 + q, [56 + j * 6 + q]))
                jobs.append(("C", j, [40 + j, 32 + j]))
                jobs.append(("A2", j, []))
                for q in range(4, 6):
                    jobs.append(("G", j * 6 + q, [56 + j * 6 + q]))
                jobs.append(("B", j, [16 + j]))
                for q, base in enumerate((8, 24, 48)):
                    jobs.append(("Z", q * 8 + j, [base + j]))
            wl = [n for jb in jobs for n in jb[2]]
            wpos = [0]

            def pf():
                if wpos[0] < len(wl):
                    prefetch(wl[wpos[0]])
                    wpos[0] += 1
            pf()
            _orig_win = win_matmul

            def win_matmul(n_idx, evac, skip0=False):
                pf()
                _orig_win(n_idx, evac, skip0)
            for kind, idx, ns in jobs:
                if kind == "A1":
                    mixer_a1(idx)
                elif kind == "A1b":
                    mixer_a1b(idx)
                elif kind == "A2":
                    mixer_a2(idx)
                    if idx == 7:
                        k.dma(cc1s[:, :], st_o[:], [k.R("st_o")], [k.R("cc1s")], engname="pool")
                        k.coll([k.R("cc1s")], [k.R("cc1d")], lambda e: e.collective_compute(
                            "AllGather", ALU.bypass, replica_groups=RG, ins=[cc1s.ap().opt()], outs=[cc1d.ap().opt()]))
                        k.dma(st_g[:], cc1d.ap().rearrange("(r p) f -> p r f", p=128), [k.R("cc1d")], [k.R("st_g")], engname="pool")
                elif kind == "G":
                    win_matmul(ns[0], evac_store(AF.Sigmoid, gs, "gs", idx), skip0=last)
                elif kind == "Z":
                    win_matmul(ns[0], evac_store(AF.Silu, zs, "zs", idx), skip0=last)
                elif kind == "C":
                    mixer_c(idx)
                elif kind == "B":
                    mixer_b(idx)

            gr_ = [k.R("st_g"), k.R("st_c"), k.R("fl")]
            k.op("dve", gr_, [k.R("st_t")], lambda e: e.tensor_tensor(out=st_t[:, 0:8], in0=st_c[:, 0:8], in1=st_g[:, 0, 0:8], op=ALU.subtract))
            k.op("dve", gr_, [k.R("st_t")], lambda e: e.tensor_tensor(out=st_t[:, 8:16], in0=st_c[:, 8:16], in1=st_g[:, 1, 8:16], op=ALU.subtract))
            k.op("dve", gr_ + [k.R("st_t")], [k.R("st_i")], lambda e: e.scalar_tensor_tensor(
                out=st_i[:, 0:8], in0=st_t[:, 0:8], scalar=fl[:, 0:1], in1=st_g[:, 0, 0:8], op0=ALU.mult, op1=ALU.add))
            k.op("dve", gr_ + [k.R("st_t")], [k.R("st_i")], lambda e: e.scalar_tensor_tensor(
                out=st_i[:, 8:16], in0=st_t[:, 8:16], scalar=fl[:, 1:2], in1=st_g[:, 1, 8:16], op0=ALU.mult, op1=ALU.add))

        def post_phase(l, last):
            tiles = list(enumerate(TMT))
            if last:
                tiles = tiles[1:]
            smr = k.R("sm")
            with contextlib.ExitStack() as so:
                pbuf = [sb("p0", [128, 8, TM], BF16, so)]
                yb = [sb("yb%d" % i, [128, 512], BF16, so) for i in range(2)]
                zb = [sb("zb%d" % i, [128, 512], BF16, so) for i in range(2)]
                ci = [0, 0, 0]

                def p_build_ops(kb, slot):
                    ops = []
                    for j in range(8):
                        for tm, (m0, n) in tiles:
                            def f(j=j, tm=tm, m0=m0, n=n):
                                p = pbuf[slot]
                                i = ci[0] % 2
                                ci[0] += 1
                                t0 = TILES[tm][0]
                                k.dma(yb[i][:, 0:n], ys[kb * 8 + j, :, m0:m0 + n], [k.R("ys", kb * 8 + j, tm)], [k.R("yb", i)])
                                k.dma(zb[i][:, 0:n], zs[kb * 8 + j, :, t0:t0 + n], [k.R("zs", kb * 8 + j, tm)], [k.R("zb", i)])
                                k.op("pool", [k.R("yb", i), k.R("zb", i)], [k.R("p", slot, tm)], lambda e: e.tensor_tensor(
                                    out=p[:, j, m0:m0 + n], in0=yb[i][:, 0:n], in1=zb[i][:, 0:n], op=ALU.mult))
                            ops.append(f)
                    return ops

                with contextlib.ExitStack() as st:
                    yt = [sb("yt%d" % i, [128, 8, 512], BF16, st) for i in range(2)]
                    sq = [sb("lsq%d" % i, [128, 512], BF16, st) for i in range(2)]
                    msb = [sb("msb%d" % i, [128, 512], F32, st) for i in range(2)]
                    vsb = [sb("vsb%d" % i, [128, 512], F32, st) for i in range(2)]
                    yn = [sb("yn%d" % i, [128, 8, 512], BF16, st) for i in range(2)]
                    pw = sb("pw", [128, 8, 1024], BF16, st)
                    ra = [[sb("ra%d_%d" % (q, i), [128, HALF], F32, st) for i in range(3)] for q in range(2)]
                    rb = [sb("rb%d" % q, [128, HALF], BF16, st) for q in range(2)]
                    for jo in range(8):
                        wb, wbr = load_w(pwc_w[l, jo, :, :], 1024, cast=False)
                        k.op("dve", [wbr], [k.R("pw")], lambda e, jo=jo, wb=wb: e.tensor_copy(out=pw[:, jo, :], in_=wb[:, 0:1024]))

                    def stage2(h):
                        q = h % 2
                        r0, r1, r2 = ra[q]
                        k.dma(r0[:], yloc[h, :, :], [k.R("yloc", h)], [k.R("ra", q, 0)], engname="pool")
                        k.dma(r1[:], afd[h, :, :], [k.R("afd", h)], [k.R("ra", q, 1)], engname="pool")
                        k.dma(r2[:], abd[h, :, :], [k.R("abd", h)], [k.R("ra", q, 2)], engname="pool")
                        k.op("dve", [k.R("ra", q, 0), k.R("ra", q, 1), k.R("st_i")], [k.R("ra", q, 0)], lambda e: e.scalar_tensor_tensor(
                            out=r0[:], in0=r1[:], scalar=st_i[:, h:h + 1], in1=r0[:], op0=ALU.mult, op1=ALU.add))
                        k.op("dve", [k.R("ra", q, 0), k.R("ra", q, 2), k.R("st_i")], [k.R("rb", q)], lambda e: e.scalar_tensor_tensor(
                            out=rb[q][:], in0=r2[:], scalar=st_i[:, 8 + h:9 + h], in1=r0[:], op0=ALU.mult, op1=ALU.add))
                        k.dma(ys[h, :, CTX:TM], rb[q][:], [k.R("rb", q)], [k.R("ys", h, tm) for tm in range(1, 5)], engname="pool")

                    lnst = {}

                    def ln_a(tm):
                        m0, n = TMT[tm]
                        b = tm % 2
                        ytr = k.R("yt", b)
                        k.dma(yt[b][:, :, 0:n], ycv[:, :, m0:m0 + n].rearrange("c p t -> p c t"), [k.R("ycv", j, tm) for j in range(8)], [ytr])
                        pm, pmr = bank()
                        pq, pqr = bank()
                        for j in range(8):
                            sqr = k.R("lsq", j % 2)
                            k.op("act", [ytr], [sqr], lambda e, j=j: e.activation(out=sq[j % 2][:, 0:n], in_=yt[b][:, j, 0:n], func=AF.Square))
                            k.op("pe", [ytr, k.R("ones_c")], [pmr], lambda e, j=j: e.matmul(
                                pm[:, 0:n], lhsT=ones_c[:], rhs=yt[b][:, j, 0:n], start=(j == 0), stop=(j == 7)))
                            k.op("pe", [sqr, k.R("ones_c")], [pqr], lambda e, j=j: e.matmul(
                                pq[:, 0:n], lhsT=ones_c[:], rhs=sq[j % 2][:, 0:n], start=(j == 0), stop=(j == 7)))
                        lnst[tm] = (pm, pmr, pq, pqr)

                    def ln_a2(tm):
                        m0, n = TMT[tm]
                        b = tm % 2
                        pm, pmr, pq, pqr = lnst[tm]
                        mr, vr = k.R("msb", b), k.R("vsb", b)
                        ms_, vs_ = msb[b], vsb[b]
                        k.op("act", [pmr], [mr], lambda e: e.activation(out=ms_[:, 0:n], in_=pm[:, 0:n], func=AF.Identity))
                        k.op("dve", [mr], [vr], lambda e: e.tensor_tensor(out=vs_[:, 0:n], in0=ms_[:, 0:n], in1=ms_[:, 0:n], op=ALU.mult))
                        k.op("dve", [pqr, vr], [vr], lambda e: e.tensor_tensor(out=vs_[:, 0:n], in0=pq[:, 0:n], in1=vs_[:, 0:n], op=ALU.subtract))
                        k.op("dve", [vr], [vr], lambda e: e.tensor_scalar(out=vs_[:, 0:n], in0=vs_[:, 0:n], scalar1=0.0, scalar2=None, op0=ALU.max))
                        k.op("act", [vr], [vr], lambda e: e.activation(out=vs_[:, 0:n], in_=vs_[:, 0:n], func=AF.Sqrt, bias=EPS))
                        k.op("dve", [vr], [vr], lambda e: e.reciprocal(out=vs_[:, 0:n], in_=vs_[:, 0:n]))

                    def ln_b(tm):
                        m0, n = TMT[tm]
                        b = tm % 2
                        ytr, mr, vr, ynr = k.R("yt", b), k.R("msb", b), k.R("vsb", b), k.R("yn", b)
                        ms_, vs_ = msb[b], vsb[b]
                        for j in range(8):
                            tf, tfr = next_etf()
                            k.op("dve", [ytr, mr], [tfr], lambda e, j=j, tf=tf: e.tensor_tensor(
                                out=tf[:, 0:n], in0=yt[b][:, j, 0:n], in1=ms_[:, 0:n], op=ALU.subtract))
                            k.op("dve", [tfr, vr], [tfr], lambda e, tf=tf: e.tensor_tensor(
                                out=tf[:, 0:n], in0=tf[:, 0:n], in1=vs_[:, 0:n], op=ALU.mult))
                            k.op("act", [tfr, smr], [ynr], lambda e, j=j, tf=tf: e.activation(
                                out=yn[b][:, j, 0:n], in_=tf[:, 0:n], func=AF.Silu, scale=smv(l, "lnc_g", j), bias=smv(l, "lnc_b", j)))

                    def ln_b2(tm):
                        m0, n = TMT[tm]
                        b = tm % 2
                        ynr = k.R("yn", b)
                        for jo in range(8):
                            ps, psr = bank()
                            pwv = pw[:, jo, :].rearrange("p (a b) -> p a b", a=8)
                            mm_group(ps, psr, [pwv[:, kc, :] for kc in range(8)], [yn[b][:, kc, 0:n] for kc in range(8)], [k.R("pw"), ynr], n)
                            tb, tbr = next_etb()
                            k.op("act", [psr, smr], [tbr], lambda e, ps=ps, tb=tb, jo=jo: e.activation(
                                out=tb[:, 0:n], in_=ps[:, 0:n], func=AF.Identity, bias=smv(l, "pwc_b", jo)))
                            k.dma(ys[16 + jo, :, m0:m0 + n], tb[:, 0:n], [tbr], [k.R("ys", 16 + jo, tm)])

                    ln_tiles = [tm for tm, _ in tiles]
                    pb_ops = p_build_ops(1, 0)
                    per_t = (len(pb_ops) + len(ln_tiles) - 1) // len(ln_tiles)
                    ln_a(ln_tiles[0])
                    ln_a2(ln_tiles[0])
                    hq = list(range(8))
                    for ii, tm in enumerate(ln_tiles):
                        if ii + 1 < len(ln_tiles):
                            ln_a(ln_tiles[ii + 1])
                        ln_b(tm)
                        if ii + 1 < len(ln_tiles):
                            ln_a2(ln_tiles[ii + 1])
                        ln_b2(tm)
                        for h in hq[ii * 2:ii * 2 + 2]:
                            stage2(h)
                        if l + 1 < L:
                            for n_ in range(32 + ii * 4, min(48, 32 + ii * 4 + 4)):
                                mod_slice(l + 1, None, n0=n_, n1=n_ + 1)
                        for f in pb_ops[ii * per_t:(ii + 1) * per_t]:
                            f()
                    for h in hq[len(ln_tiles) * 2:]:
                        stage2(h)
                    k.barrier()

                with contextlib.ExitStack() as st:
                    pbuf.append(sb("p1", [128, 8, TM], BF16, st))
                    acc = sb("acc", [128, NCH, TM], BF16, st)
                    gb = [sb("gb%d" % i, [128, 512], BF16, st) for i in range(3)]
                    xb = [sb("xb%d" % i, [128, 512], F32, st) for i in range(4)]
                    order = [(1, 0), (0, 1), (2, 0)]
                    for oi, (kb, slot) in enumerate(order):
                        p = pbuf[slot]
                        nxt_ops = p_build_ops(*order[oi + 1]) if oi + 1 < len(order) else []
                        per_c = (len(nxt_ops) + NCH - 1) // NCH
                        nxt = load_w(w_bout[l, kb, 0, :, :], 1024, cast_eng="act")
                        for c in range(NCH):
                            wb, wbr = nxt
                            if c + 1 < NCH:
                                nxt = load_w(w_bout[l, kb, c + 1, :, :], 1024, cast_eng="act")
                            wv = wb[:, 0:1024].rearrange("p (a b) -> p a b", a=8)
                            for tm, (m0, n) in tiles:
                                i = ci[1] % 3
                                ci[1] += 1
                                t0 = TILES[tm][0]
                                k.dma(gb[i][:, 0:n], gs[kb * 16 + c, :, t0:t0 + n], [k.R("gs", kb * 16 + c, tm)], [k.R("gb", i)])
                                ps, psr = bank()
                                mm_group(ps, psr, [wv[:, kc, :] for kc in range(8)], [p[:, kc, m0:m0 + n] for kc in range(8)],
                                         [wbr, k.R("p", slot, tm)], n)
                                ar = k.R("acc", c, tm)
                                if oi == 0:
                                    k.op("dve", [psr, k.R("gb", i)], [ar], lambda e, ps=ps, i=i, c=c, m0=m0, n=n: e.tensor_tensor(
                                        out=acc[:, c, m0:m0 + n], in0=ps[:, 0:n], in1=gb[i][:, 0:n], op=ALU.mult))
                                else:
                                    tf, tfr = next_etf()
                                    k.op("dve", [psr, k.R("gb", i)], [tfr], lambda e, ps=ps, i=i, tf=tf, n=n: e.tensor_tensor(
                                        out=tf[:, 0:n], in0=ps[:, 0:n], in1=gb[i][:, 0:n], op=ALU.mult))
                                    k.op("dve", [tfr, ar], [ar], lambda e, tf=tf, c=c, m0=m0, n=n: e.tensor_tensor(
                                        out=acc[:, c, m0:m0 + n], in0=tf[:, 0:n], in1=acc[:, c, m0:m0 + n], op=ALU.add))
                            for f in nxt_ops[c * per_c:(c + 1) * per_c]:
                                f()
                    its = [(c, tm, m0, n) for c in range(NCH) for tm, (m0, n) in tiles]

                    def xload(it):
                        c, tm, m0, n = its[it]
                        t0 = 0 if tm == 0 else TILES[tm][0]
                        src_ = xs if l == 0 else xres
                        k.dma(xb[it % 4][:, 0:n], src_[c, :, t0:t0 + n], [] if l == 0 else [k.R("xres", c, tm)], [k.R("xb", it % 4)])
                    xload(0)
                    xload(1)
                    nxt = load_w(w_out[l, 0, :, :], 2048, cast_eng="act")
                    for it, (c, tm, m0, n) in enumerate(its):
                        if tm == tiles[0][0]:
                            wb, wbr = nxt
                            if c + 1 < NCH:
                                nxt = load_w(w_out[l, c + 1, :, :], 2048, cast_eng="act")
                            wv = wb[:].rearrange("p (a b) -> p a b", a=16)
                        if it + 2 < len(its):
                            xload(it + 2)
                        i = it % 4
                        if tm == 0:
                            t0, mi = 0, 4
                        else:
                            t0, mi = TILES[tm][0], 0
                        xr = k.R("xres", c, tm)
                        ps, psr = bank()
                        mm_group(ps, psr, [wv[:, kc, :] for kc in range(16)], [acc[:, kc, m0:m0 + n] for kc in range(16)],
                                 [wbr] + [k.R("acc", kc, tm) for kc in range(16)], n)
                        tf, tfr = next_etf()
                        k.op("act", [psr, k.R("der", l)], [tfr], lambda e, ps=ps, tf=tf, c=c, mi=mi, n=n: e.activation(
                            out=tf[:, 0:n], in_=ps[:, 0:n], func=AF.Identity,
                            scale=der[:, l, mi + 2, c:c + 1], bias=der[:, l, mi + 3, c:c + 1]))
                        k.op("dve", [tfr, k.R("xb", i)], [k.R("xb", i)], lambda e, tf=tf, i=i, n=n: e.tensor_tensor(
                            out=xb[i][:, 0:n], in0=xb[i][:, 0:n], in1=tf[:, 0:n], op=ALU.add))
                        k.dma(xres[c, :, t0:t0 + n], xb[i][:, 0:n], [k.R("xb", i)], [xr])
                    k.barrier()

        def halo_exchange():
            allx = lambda tt: [k.R("xres", c, tt) for c in range(NCH)]
            hv = hxs.ap().rearrange("(c p) f -> c p f", p=128)
            k.dma(hv[:, :, 0:16], xres[:, :, P_OWN - 16:P_OWN], allx(1), [k.R("hxs")], engname="pool")
            k.dma(hv[:, :, 16:32], xres[:, :, T - 16:T], allx(4), [k.R("hxs")], engname="pool")
            k.coll([k.R("hxs")], [k.R("hxd")], lambda e: e.collective_compute(
                "AllGather", ALU.bypass, replica_groups=RG, ins=[hxs.ap().opt()], outs=[hxd.ap().opt()]))
            dv = hxd.ap().rearrange("(r c p) f -> r c p f", r=2, p=128)
            k.dma(xres[:, :, CTX:CTX + 16], dv[0, :, :, 16:32], [k.R("hxd")], allx(0), engname="pool")
            k.dma(xres[:, :, CTX + 16:CTX + 32], dv[1, :, :, 0:16], [k.R("hxd")], allx(0), engname="pool")

        assert TILES[1][0] == CTX + 2 * HALO

        for l in range(L):
            last = (l == L - 1)
            with contextlib.ExitStack() as st:
                layer_front(st, l, last)
                k.barrier()
            post_phase(l, last)
            if not last:
                halo_exchange()
        with contextlib.ExitStack() as st:
            norm_phase(st, L - 1, True, None)
        k.barrier()
    return nc


def _tile_w(w, nk, nn):
    return np.ascontiguousarray(w.reshape(nk, 128, nn, 128).transpose(2, 1, 0, 3).reshape(nn, 128, nk * 128))


def _vec(v, n):
    return np.ascontiguousarray(v.reshape(n, 128).T)


def _prep_shared(inp, L):
    f = np.float32
    sh = {}
    sh["w_ada"] = np.stack([_tile_w(inp["w_ada"][l], 16, 48) for l in range(L)])
    sh["w_in"] = np.stack([_tile_w(inp["w_in"][l], 16, NW_IN) for l in range(L)])
    sh["pwc_w"] = np.stack([_tile_w(inp["pwc_w"][l], 8, 8) for l in range(L)])
    sh["w_out"] = np.stack([_tile_w(inp["w_out"][l], 16, 16) for l in range(L)])
    sh["w_bout"] = np.stack([np.stack([_tile_w(inp["w_bout"][l][kb], 8, 16) for kb in range(3)]) for l in range(L)])
    pw = np.zeros((L, 8, 128, 256), f)
    for l in range(L):
        for jo in range(8):
            g, jj = jo // 2, jo % 2
            blk = inp["pool_w"][l][g][:, jj * 128:(jj + 1) * 128]
            pw[l, jo] = blk.reshape(2, 128, 128).transpose(1, 0, 2).reshape(128, 256)
    sh["pool_w"] = pw
    sh["ident"] = np.eye(128, dtype=f)
    rcl = np.zeros((4, 64), f)
    rcc = np.zeros((4, 256), f)
    for g, w in enumerate(POOL_W):
        for n_, tab in ((64, rcl), (256, rcc)):
            t = np.arange(n_)
            lo = np.maximum(t - w // 2, 0)
            hi = np.minimum(t + w // 2, n_)
            tab[g] = 1.0 / (hi - lo)
    sh["rcl"] = np.ascontiguousarray(np.broadcast_to(rcl.reshape(1, -1), (128, 256)))
    sh["rcc"] = np.ascontiguousarray(np.broadcast_to(rcc.reshape(1, -1), (128, 1024)))
    sh["final_g"] = _vec(inp["final_g"], 16)
    return sh


def _prep_parity(inp, L, par):
    f = np.float32
    wg = np.zeros((L, 8, 128, 512), f)
    sm = np.zeros((128, L * NSM), f)
    for l in range(L):
        for h in range(8):
            for d in range(2):
                wg[l, h, :, (d * 2 + 0) * 128:(d * 2 + 1) * 128] = inp["lru_wr"][l][d][h]
                wg[l, h, :, (d * 2 + 1) * 128:(d * 2 + 2) * 128] = inp["lru_wi"][l][d][h]
        def put(name, arr):
            o, s = SM[name]
            assert arr.shape == (128, s), (name, arr.shape)
            sm[:, l * NSM + o:l * NSM + o + s] = arr
        put("norm_g", _vec(inp["norm_g"][l], 16))
        put("b_out", _vec(inp["b_out"][l], 16))
        put("b_ada", _vec(inp["b_ada"][l], 48))
        cw = np.zeros((128, 8, 5), f)
        cw[:, :, 0:4] = inp["conv_a_w"][l].reshape(4, 8, 128).transpose(2, 1, 0)
        put("cwA", cw.reshape(128, 40))
        put("cbA", _vec(inp["conv_a_b"][l], 8))
        bg = np.zeros((128, 8, 4), f)
        for d in range(2):
            bg[:, :, d * 2 + 0] = inp["lru_br"][l][d].T
            bg[:, :, d * 2 + 1] = inp["lru_bi"][l][d].T
        put("bg", bg.reshape(128, 32))
        lam = np.zeros((128, 8, 2), f)
        for d in range(2):
            lam[:, :, d] = inp["lru_lambda"][l][d].reshape(8, 128).T
        put("lam", lam.reshape(128, 16))
        put("pool_scale", _vec(inp["pool_scale"][l], 8))
        put("cwC", np.ascontiguousarray(inp["convc_w"][l].reshape(31, 8, 128).transpose(2, 1, 0)).reshape(128, 248))
        put("cbC", _vec(inp["convc_b"][l], 8))
        put("lnc_g", _vec(inp["lnc_g"][l], 8))
        put("lnc_b", _vec(inp["lnc_b"][l], 8))
        put("pwc_b", _vec(inp["pwc_b"][l], 8))
    return {"wg": wg, "sm": sm}


def _prep_core(inp, c):
    f = np.float32
    b, half = c // 2, c % 2
    x = inp["x"][b]
    own = x[half * HALF:(half + 1) * HALF]
    hl = x[half * HALF - HALO:half * HALF] if half == 1 else np.zeros((HALO, D), f)
    hr = x[(half + 1) * HALF:(half + 1) * HALF + HALO] if half == 0 else np.zeros((HALO, D), f)
    tok = np.concatenate([inp["ctx"][b], hl, hr, own], axis=0)
    xs = np.ascontiguousarray(tok.T).reshape(NCH, 128, T)
    cv = np.zeros((128, 16, 2), f)
    cv[:, :, 0] = inp["c"][b].reshape(16, 128).T
    cv[:, :, 1] = inp["c_ctx"].reshape(16, 128).T
    e = 1.0 if half == 0 else 0.0
    fl = np.tile(np.array([e, 1 - e, 1 - e, e], f).reshape(1, 4), (128, 1))
    return {"xs": xs, "cvec": cv.reshape(128, 32), "flags": np.ascontiguousarray(fl)}


def run(inp, L=4):
    inp = {k_: np.asarray(v, dtype=np.float32) for k_, v in inp.items()}
    nc = build_program(L)
    shared = _prep_shared(inp, L)
    shared.update(_prep_parity(inp, L, 0))
    in_maps = []
    for c in range(8):
        m = dict(shared)
        m.update(_prep_core(inp, c))
        in_maps.append(m)
    res = run_bass_kernel_spmd(nc, in_maps, core_ids=list(range(8)))
    outp = np.zeros((4, SEQ, D), np.float32)
    for c in range(8):
        o = np.asarray(res.results[c]["out"]).reshape(D, HALF)
        outp[c // 2, (c % 2) * HALF:(c % 2 + 1) * HALF] = o.T
    return outp


def kernel(**inputs):
    return run(inputs, 4)
```
